# Optimizing a Trainium2 kernel written in Bass

```python
import jax, jax.numpy as jnp
from jax import lax
import numpy as np

D_MODEL = 2048
BATCH = 8
SEQ = 2048
DEPTH = 4

N_MIXERS = 3
GRID_W = 64
HEAD_DIM = 128
EPS = 1e-6
NEG_INF = -1e30
NA_HEADS = D_MODEL // HEAD_DIM
NA_WIN_R = 8
NA_WIN_C = 16
SC_WIDTH = 3
GQA_Q_HEADS = D_MODEL // HEAD_DIM
GQA_KV_HEADS = GQA_Q_HEADS // 4
GQA_GROUP = GQA_Q_HEADS // GQA_KV_HEADS
Q_BLOCK = 128
ROPE_THETA = 10000.0
D_FF = ((8 * D_MODEL // 3 + 255) // 256) * 256
FFN_CONV_WIDTH = 3

kernel_name = "hybrid_natten_shortconv_gqa_convffn_encoder"


def rms_norm(x, g):
    xf = x.astype(jnp.float32)
    y = xf * lax.rsqrt(jnp.mean(xf * xf, axis=-1, keepdims=True) + EPS)
    return (y * g.astype(jnp.float32)).astype(x.dtype)


def dwconv_centered(x, w):
    k = w.shape[0]
    pad = k // 2
    s = x.shape[1]
    xp = jnp.pad(x, ((0, 0), (pad, k - 1 - pad), (0, 0)))
    out = xp[:, 0:s] * w[0]
    for i in range(1, k):
        out = out + xp[:, i:i + s] * w[i]
    return out


def neighborhood_attention(x, w_qkv, rpb, w_o):
    b, s, _ = x.shape
    rows = s // GRID_W
    wr = min(NA_WIN_R, rows)
    q, k, v = jnp.split(x @ w_qkv, 3, axis=-1)
    grid = lambda t: t.reshape(b, rows, GRID_W, NA_HEADS, HEAD_DIM)
    q = grid(q) * (HEAD_DIM ** -0.5)
    k, v = grid(k), grid(v)
    cols = jnp.arange(GRID_W)
    c_start = jnp.clip(cols - NA_WIN_C // 2, 0, GRID_W - NA_WIN_C)
    col_valid = (cols[None, :] >= c_start[:, None]) & (cols[None, :] < c_start[:, None] + NA_WIN_C)
    dc = jnp.clip(cols[None, :] - cols[:, None] + NA_WIN_C - 1, 0, 2 * NA_WIN_C - 2)
    rpb_cols = rpb[:, :, dc]

    def row_block(r):
        r_start = jnp.clip(r - NA_WIN_R // 2, 0, rows - wr)
        k_r = lax.dynamic_slice_in_dim(k, r_start, wr, axis=1)
        v_r = lax.dynamic_slice_in_dim(v, r_start, wr, axis=1)
        q_r = lax.dynamic_index_in_dim(q, r, axis=1, keepdims=False)
        dr = r_start + jnp.arange(wr) - r + NA_WIN_R - 1
        bias = jnp.transpose(rpb_cols[:, dr], (0, 2, 1, 3))
        sc = jnp.einsum('bqhd,bikhd->bhqik', q_r, k_r, preferred_element_type=jnp.float32)
        sc = sc + bias[None].astype(jnp.float32)
        sc = jnp.where(col_valid[:, None, :], sc, NEG_INF)
        p = jax.nn.softmax(sc.reshape(b, NA_HEADS, GRID_W, wr * GRID_W), axis=-1)
        p = p.reshape(b, NA_HEADS, GRID_W, wr, GRID_W).astype(v.dtype)
        return jnp.einsum('bhqik,bikhd->bqhd', p, v_r)

    out = lax.map(row_block, jnp.arange(rows))
    out = jnp.transpose(out, (1, 0, 2, 3, 4)).reshape(b, s, NA_HEADS * HEAD_DIM)
    return out @ w_o


def short_conv_mixer(x, w_in, conv_w, w_out):
    gb, gc, h = jnp.split(x @ w_in, 3, axis=-1)
    return (gb * dwconv_centered(gc * h, conv_w)) @ w_out


def axial_rope_tables(s):
    t = jnp.arange(s)
    row = (t // GRID_W).astype(jnp.float32)[:, None]
    col = (t % GRID_W).astype(jnp.float32)[:, None]
    half = HEAD_DIM // 2
    inv = ROPE_THETA ** (-jnp.arange(0, half, 2, dtype=jnp.float32) / half)
    ang = jnp.concatenate([row * inv, row * inv, col * inv, col * inv], axis=-1)
    return jnp.cos(ang), jnp.sin(ang)


def rotate_half_axial(x):
    lead = x.shape[:-1]
    xs = x.reshape(*lead, 2, 2, HEAD_DIM // 4)
    x1, x2 = xs[..., 0, :], xs[..., 1, :]
    return jnp.stack([-x2, x1], axis=-2).reshape(*lead, HEAD_DIM)


def apply_axial_rope(x, cos, sin):
    xf = x.astype(jnp.float32)
    return (xf * cos[:, None, :] + rotate_half_axial(xf) * sin[:, None, :]).astype(x.dtype)


def gqa_axial_attention(x, w_qkv, q_norm, k_norm, w_o):
    b, s, _ = x.shape
    qkv = x @ w_qkv
    nq = GQA_Q_HEADS * HEAD_DIM
    nk = GQA_KV_HEADS * HEAD_DIM
    q = qkv[..., :nq].reshape(b, s, GQA_Q_HEADS, HEAD_DIM)
    k = qkv[..., nq:nq + nk].reshape(b, s, GQA_KV_HEADS, HEAD_DIM)
    v = qkv[..., nq + nk:].reshape(b, s, GQA_KV_HEADS, HEAD_DIM)
    cos, sin = axial_rope_tables(s)
    q = apply_axial_rope(rms_norm(q, q_norm), cos, sin) * (HEAD_DIM ** -0.5)
    k = apply_axial_rope(rms_norm(k, k_norm), cos, sin)
    qb = q.reshape(b, s // Q_BLOCK, Q_BLOCK, GQA_KV_HEADS, GQA_GROUP, HEAD_DIM)

    def block(q_blk):
        sc = jnp.einsum('bqkgd,bskd->bkgqs', q_blk, k, preferred_element_type=jnp.float32)
        p = jax.nn.softmax(sc, axis=-1).astype(v.dtype)
        return jnp.einsum('bkgqs,bskd->bqkgd', p, v)

    out = lax.map(block, jnp.moveaxis(qb, 1, 0))
    out = jnp.moveaxis(out, 0, 1).reshape(b, s, nq)
    return out @ w_o


def conv_glu_ffn(x, w_up, conv_w, conv_b, w_down):
    h = dwconv_centered(x @ w_up, conv_w) + conv_b
    g, u = jnp.split(h, 2, axis=-1)
    return (jax.nn.silu(g) * u) @ w_down


def setup_inputs(seed: int = 0) -> dict:
    key = jax.random.key(seed)
    ks = iter(jax.random.split(key, 32))
    n_a, n_b, n_c = (len(range(m, DEPTH, N_MIXERS)) for m in range(N_MIXERS))

    def w(shape, fan_in):
        return jax.random.normal(next(ks), shape, jnp.float32) * (fan_in ** -0.5)

    def gain(shape):
        return 1.0 + 0.02 * jax.random.normal(next(ks), shape, jnp.float32)

    gqa_cols = (GQA_Q_HEADS + 2 * GQA_KV_HEADS) * HEAD_DIM
    return {
        "x": jax.random.normal(next(ks), (BATCH, SEQ, D_MODEL), jnp.float32),
        "mix_norm": gain((DEPTH, D_MODEL)),
        "ffn_norm": gain((DEPTH, D_MODEL)),
        "final_norm": gain((D_MODEL,)),
        "na_w_qkv": w((n_a, D_MODEL, 3 * NA_HEADS * HEAD_DIM), D_MODEL),
        "na_rpb": 0.1 * jax.random.normal(next(ks), (n_a, NA_HEADS, 2 * NA_WIN_R - 1, 2 * NA_WIN_C - 1), jnp.float32),
        "na_w_o": w((n_a, NA_HEADS * HEAD_DIM, D_MODEL), NA_HEADS * HEAD_DIM),
        "sc_w_in": w((n_b, D_MODEL, 3 * D_MODEL), D_MODEL),
        "sc_conv_w": w((n_b, SC_WIDTH, D_MODEL), SC_WIDTH),
        "sc_w_out": w((n_b, D_MODEL, D_MODEL), D_MODEL),
        "gqa_w_qkv": w((n_c, D_MODEL, gqa_cols), D_MODEL),
        "gqa_q_norm": gain((n_c, HEAD_DIM)),
        "gqa_k_norm": gain((n_c, HEAD_DIM)),
        "gqa_w_o": w((n_c, GQA_Q_HEADS * HEAD_DIM, D_MODEL), GQA_Q_HEADS * HEAD_DIM),
        "ffn_w_up": w((DEPTH, D_MODEL, 2 * D_FF), D_MODEL),
        "ffn_conv_w": w((DEPTH, FFN_CONV_WIDTH, 2 * D_FF), FFN_CONV_WIDTH),
        "ffn_conv_b": 0.01 * jax.random.normal(next(ks), (DEPTH, 2 * D_FF), jnp.float32),
        "ffn_w_down": w((DEPTH, D_FF, D_MODEL), D_FF),
    }


def reference(x, mix_norm, ffn_norm, final_norm, na_w_qkv, na_rpb, na_w_o,
              sc_w_in, sc_conv_w, sc_w_out, gqa_w_qkv, gqa_q_norm, gqa_k_norm,
              gqa_w_o, ffn_w_up, ffn_conv_w, ffn_conv_b, ffn_w_down):
    h = x
    for i in range(DEPTH):
        m, j = i % N_MIXERS, i // N_MIXERS
        a = rms_norm(h, mix_norm[i])
        if m == 0:
            mixed = neighborhood_attention(a, na_w_qkv[j], na_rpb[j], na_w_o[j])
        elif m == 1:
            mixed = short_conv_mixer(a, sc_w_in[j], sc_conv_w[j], sc_w_out[j])
        else:
            mixed = gqa_axial_attention(a, gqa_w_qkv[j], gqa_q_norm[j], gqa_k_norm[j], gqa_w_o[j])
        h = h + mixed
        h = h + conv_glu_ffn(rms_norm(h, ffn_norm[i]), ffn_w_up[i], ffn_conv_w[i],
                             ffn_conv_b[i], ffn_w_down[i])
    return rms_norm(h, final_norm)
```

```python
import numpy as np
import concourse.bass as bass
import concourse.mybir as mybir
from concourse.bass_utils import run_bass_kernel_spmd
from contextlib import ExitStack

F32 = mybir.dt.float32
BF16 = mybir.dt.bfloat16
AF = mybir.ActivationFunctionType
ALU = mybir.AluOpType


class _Op:
    __slots__ = ("eng", "fn", "deps", "needed", "sig", "is_dma")

    def __init__(self, eng, fn, deps, is_dma):
        self.eng = eng
        self.fn = fn
        self.deps = deps
        self.needed = False
        self.sig = None
        self.is_dma = is_dma


class _DSem:
    def __init__(self, sem):
        self.sem = sem
        self.count = 0


class Prog:
    ENG = ("pe", "act", "dve", "pool", "sp")
    SEM_LIMIT = 30000

    def __init__(self, nc, es):
        self.nc = nc
        self.es = es
        self.ops = {e: [] for e in self.ENG}
        self.res = {}
        self.esem = {e: es.enter_context(nc.semaphore("sem_" + e)) for e in self.ENG}
        self.nwaits = 0
        self.last_real = {}
        self.pending_dma = []

    def sbuf(self, name, shape, dt):
        return self.es.enter_context(self.nc.sbuf_tensor(name, shape, dt))

    def psum(self, name, shape, dt):
        return self.es.enter_context(self.nc.psum_tensor(name, shape, dt))

    def dsem(self, name):
        return _DSem(self.es.enter_context(self.nc.semaphore(name)))

    def _record(self, eng, fn, reads, writes, is_dma):
        deps = []
        for r in reads:
            st = self.res.get(r)
            if st is not None and st[0] is not None:
                deps.append(st[0])
        for w in writes:
            st = self.res.get(w)
            if st is not None:
                if st[0] is not None:
                    deps.append(st[0])
                deps.extend(st[1].values())
                deps.extend(st[2])
        op = _Op(eng, fn, deps, is_dma)
        for d in deps:
            if not (d.eng == "pe" and eng == "pe" and not d.is_dma and not is_dma):
                d.needed = True
        for r in reads:
            st = self.res.get(r)
            if st is None:
                st = self.res[r] = [None, {}, []]
            if is_dma:
                st[2].append(op)
            else:
                st[1][eng] = op
        for w in writes:
            self.res[w] = [op, {}, []]
        self.ops[eng].append(op)
        if fn is not None:
            if is_dma:
                self.pending_dma.append(op)
            else:
                self.last_real[eng] = op
        return op

    def op(self, eng, fn, reads=(), writes=()):
        return self._record(eng, fn, reads, writes, False)

    def dma(self, eng, out, in_, dsem, reads=(), writes=()):
        op = self._record(eng, lambda e: e.dma_start(out=out, in_=in_), reads, writes, True)
        if dsem.count + 16 > self.SEM_LIMIT:
            dsem.sem = self.es.enter_context(self.nc.semaphore())
            dsem.count = 0
        dsem.count += 16
        op.sig = (dsem.sem, dsem.count)
        return op

    def finish(self, final_waits=()):
        self.op("sp", None, reads=list(final_waits))
        for e in self.ENG:
            c = 0
            sem = self.esem[e]
            for op in self.ops[e]:
                if not op.is_dma and op.needed:
                    if c >= self.SEM_LIMIT:
                        sem = self.es.enter_context(self.nc.semaphore())
                        c = 0
                    c += 1
                    op.sig = (sem, c)
        engs = {"pe": "tensor", "act": "scalar", "dve": "vector", "pool": "gpsimd", "sp": "sync"}
        with self.nc.Block() as block:
            for e in self.ENG:
                getattr(block, engs[e])(lambda eng, e=e: self._emit(e, eng))

    def _emit(self, e, eng):
        waited = {}
        for op in self.ops[e]:
            for d in op.deps:
                if d.eng == "pe" and e == "pe" and not d.is_dma and not op.is_dma:
                    continue
                sem, val = d.sig
                key = id(sem)
                if waited.get(key, 0) >= val:
                    continue
                eng.wait_ge(sem, val)
                self.nwaits += 1
                waited[key] = val
            if op.fn is None:
                continue
            ins = op.fn(eng)
            if op.is_dma:
                ins.then_inc(op.sig[0], 16)
            elif op.needed:
                ins.then_inc(op.sig[0], 1)

    def barrier(self):
        lasts = [self.last_real[e] for e in self.ENG if self.last_real.get(e) is not None]
        lasts += self.pending_dma
        for e in self.ENG:
            op = _Op(e, None, list(lasts), False)
            self.ops[e].append(op)
        for d in lasts:
            d.needed = True
        self.pending_dma = []
        self.res = {}


S = 2048
D = 2048
DFF = 5632
NH = 16
EPS = 1e-6
NBLK = 8
BT = 256
NCHF = DFF // 128

CV_MIX = 0
CV_FFN = 64
CV_FIN = 128
CV_FCW = 144
CV_FCB = CV_FCW + 4 * 3 * 88
CV_SCW = CV_FCB + 4 * 88
CV_QN = CV_SCW + 48
CV_KN = CV_QN + 1
NCV = CV_KN + 1

BIG_B = 90112
RSLOT_B = 12288
NRING = 4
SCR_OFF = BIG_B + NRING * RSLOT_B
M_BYTES = 204800

NA_VAR = {0: (0, 4), 1: (4, 4), "int": (8, 5), 14: (13, 4), 15: (17, 4)}
NA_NTAB = 21


def _na_pair(m):
    if m in (0, 1):
        tb, nch = NA_VAR[m]
        return tb, nch, 0
    if m in (14, 15):
        tb, nch = NA_VAR[m]
        return tb, nch, 12
    tb, nch = NA_VAR["int"]
    return tb, nch, m - 2


class Builder:
    def __init__(self, cfg):
        self.cfg = cfg
        nc = self.nc = bass.Bass("TRN2", target_bir_lowering=False)
        self.es = ExitStack()
        p = self.p = Prog(nc, self.es)

        def din(name, shape, dt=F32):
            return nc.dram_tensor(name, shape, dt, kind="ExternalInput").ap()

        def dscr(name, shape, dt):
            return nc.dram_tensor(name, shape, dt, kind="Internal").ap()

        self.x = din("x", [S, D])
        self.cv = din("cv", [128, NCV])
        self.ident = din("ident", [128, 128])
        self.rot = din("rot", [128, 128])
        self.cs = din("cs", [2, 128, S])
        self.w_up = din("w_up", [4, 44, 128, 4096])
        self.w_dn = din("w_dn", [4, 16, 128, 5632])
        self.na_qkv = din("na_qkv", [2, 24, 128, 4096])
        self.na_o = din("na_o", [2, 8, 128, 4096])
        self.na_bias = din("na_bias", [2, NH, 128, NA_NTAB * 128])
        self.sc_in = din("sc_in", [24, 128, 4096])
        self.sc_o = din("sc_o", [8, 128, 4096])
        self.gq_qkv = din("gq_qkv", [12, 128, 4096])
        self.gq_o = din("gq_o", [8, 128, 4096])
        self.out = nc.dram_tensor("out", [S, D], F32, kind="ExternalOutput").ap()
        self.hT = dscr("hT", [16, 128, S], F32)
        self.hm = dscr("hm", [NCHF, 128, S], BF16)
        self.mo = dscr("mo", [16, 128, S], BF16)
        self.qs = dscr("qs", [16, 128, S], BF16)
        if cfg.get("dump_h"):
            self.hdump = nc.dram_tensor("hdump", [16, 128, S], F32, kind="ExternalOutput").ap()

        self.M = p.sbuf("M", [128, M_BYTES // 4], F32)
        self.cvt = p.sbuf("cvt", [128, NCV], F32)
        self.identf = p.sbuf("identf", [128, 128], F32)
        self.identb = p.sbuf("identb", [128, 128], BF16)
        self.onesb = p.sbuf("onesb", [128, 128], BF16)
        self.rotb = p.sbuf("rotb", [128, 128], BF16)
        self.epsb = p.sbuf("epsb", [128, 1], F32)
        self.banks = [p.psum("ps%d" % i, [128, 512], F32) for i in range(8)]
        self.bank_i = 0
        self.sems = {}
        self.ring_i = 0

    def V(self, off, dt, shape):
        n = 1
        for s_ in shape[1:]:
            n *= s_
        esz = 4 if dt == F32 else 2
        assert off % 4 == 0 and (n * esz) % 4 == 0
        assert off + n * esz <= M_BYTES, (off, n, esz)
        ap = self.M[:, off // 4:(off + n * esz) // 4]
        if dt != F32:
            ap = ap.bitcast(dt)
        if len(shape) == 3:
            ap = ap.rearrange("p (a b) -> p a b", b=shape[2])
        return ap

    def sem(self, name):
        if name not in self.sems:
            self.sems[name] = self.p.dsem("d_" + name.replace(":", "_"))
        return self.sems[name]

    def bank(self):
        i = self.bank_i
        self.bank_i = (i + 1) % 8
        return self.banks[i], "ps%d" % i

    def ring_slot(self):
        i = self.ring_i
        self.ring_i = (i + 1) % NRING
        return i

    def load_slab(self, src, nel):
        i = self.ring_slot()
        name = "ring%d" % i
        v = self.V(BIG_B + i * RSLOT_B, BF16, [128, nel])
        self.p.dma("pool", v, src, self.sem(name), reads=[], writes=[name])
        return v, name

    def consts(self):
        p = self.p
        p.dma("sp", self.cvt[:], self.cv, self.sem("cvt"), writes=["cvt"])
        p.dma("sp", self.identf[:], self.ident, self.sem("identf"), writes=["identf"])
        p.dma("pool", self.identb[:], self.ident, self.sem("identb"), writes=["identb"])
        p.dma("pool", self.rotb[:], self.rot, self.sem("rotb"), writes=["rotb"])
        p.op("pool", lambda e: e.memset(self.onesb[:], 1.0), writes=["onesb"])
        p.op("pool", lambda e: e.memset(self.epsb[:], EPS), writes=["epsb"])
        p.barrier()

    def norm_block(self, hblk, hname, b, gcol, out_ap, out_name, sqv, sdv, rsv, scale_dim=D):
        p = self.p
        ones, cvt, epsb = self.onesb, self.cvt, self.epsb
        p.op("act", lambda e: e.activation(sqv, hblk, AF.Square), reads=[hname], writes=["sq"])
        bk, bname = self.bank()
        for c in range(16):
            p.op("pe", lambda e, c=c: e.matmul(bk[:, 0:BT], ones[:], sqv[:, c, :], start=(c == 0), stop=(c == 15)),
                 reads=["sq", "onesb"], writes=[bname])
        p.op("act", lambda e: e.activation(sdv, bk[:, 0:BT], AF.Sqrt, bias=epsb[:, 0:1], scale=1.0 / scale_dim),
             reads=[bname, "epsb"], writes=["sd"])
        p.op("dve", lambda e: e.reciprocal(rsv, sdv), reads=["sd"], writes=["rs"])
        for c in range(16):
            p.op("dve", lambda e, c=c: e.scalar_tensor_tensor(
                out_ap[:, c, :], hblk[:, c, :], cvt[:, gcol + c:gcol + c + 1], rsv, ALU.mult, ALU.mult),
                reads=[hname, "rs", "cvt"], writes=[out_name])

    def scr_norm(self):
        o = SCR_OFF
        hb = [self.V(o + i * 16384, F32, [128, 16, BT]) for i in range(2)]
        o += 32768
        xt = self.V(o, F32, [128, 2048]); o += 8192
        sq = self.V(o, BF16, [128, 16, BT]); o += 8192
        sd = self.V(o, F32, [128, BT]); o += 1024
        rs = self.V(o, F32, [128, BT]); o += 1024
        return hb, xt, sq, sd, rs

    def xn_view(self):
        return self.V(0, BF16, [128, 16, S])

    def phase_input(self, gcol):
        p = self.p
        hb, xt, sq, sd, rs = self.scr_norm()
        xn = self.xn_view()
        identf = self.identf
        for b in range(NBLK):
            hblk, hname = hb[b % 2], "hb%d" % (b % 2)
            for tl in range(2):
                r0 = (b * 2 + tl) * 128
                p.dma("sp", xt, self.x[r0:r0 + 128, :], self.sem("xt"), writes=["xt"])
                for cg in range(4):
                    bk, bname = self.bank()
                    for ci in range(4):
                        c = cg * 4 + ci
                        p.op("pe", lambda e, c=c, ci=ci, bk=bk: e.transpose(
                            bk[:, ci * 128:(ci + 1) * 128], xt[:, c * 128:(c + 1) * 128], identf[:]),
                            reads=["xt", "identf"], writes=[bname])
                    eng = "act" if cg % 2 == 0 else "dve"
                    dst = hblk[:, cg * 4:(cg + 1) * 4, tl * 128:(tl + 1) * 128]
                    src = bk[:, :].rearrange("p (a b) -> p a b", b=128)
                    if eng == "act":
                        p.op("act", lambda e, dst=dst, src=src: e.activation(dst, src, AF.Copy),
                             reads=[bname], writes=[hname])
                    else:
                        p.op("dve", lambda e, dst=dst, src=src: e.tensor_copy(dst, src),
                             reads=[bname], writes=[hname])
            p.dma("sp", self.hT[:, :, b * BT:(b + 1) * BT].rearrange("c p t -> p c t"), hblk,
                  self.sem(hname + "st"), reads=[hname], writes=["hT:b%d" % b])
            self.norm_block(hblk, hname, b, gcol, xn[:, :, b * BT:(b + 1) * BT], "xn:%d" % b, sq, sd, rs)
        p.barrier()

    def phase_norm(self, gcol):
        p = self.p
        hb, xt, sq, sd, rs = self.scr_norm()
        xn = self.xn_view()
        for b in range(NBLK):
            hblk, hname = hb[b % 2], "hb%d" % (b % 2)
            p.dma("sp", hblk, self.hT[:, :, b * BT:(b + 1) * BT].rearrange("c p t -> p c t"),
                  self.sem(hname), reads=[], writes=[hname])
            self.norm_block(hblk, hname, b, gcol, xn[:, :, b * BT:(b + 1) * BT], "xn:%d" % b, sq, sd, rs)
        p.barrier()

    def phase_output(self, gcol):
        p = self.p
        hb, xt, sq, sd, rs = self.scr_norm()
        identf = self.identf
        ot = [self.V(0 + i * 8192, F32, [128, 2048]) for i in range(2)]
        k = 0
        for b in range(NBLK):
            hblk, hname = hb[b % 2], "hb%d" % (b % 2)
            p.dma("sp", hblk, self.hT[:, :, b * BT:(b + 1) * BT].rearrange("c p t -> p c t"),
                  self.sem(hname), reads=[], writes=[hname])
            self.norm_block(hblk, hname, b, gcol, hblk, hname, sq, sd, rs)
            for tl in range(2):
                otile, oname = ot[k % 2], "ot%d" % (k % 2)
                k += 1
                for cg in range(4):
                    bk, bname = self.bank()
                    for ci in range(4):
                        c = cg * 4 + ci
                        p.op("pe", lambda e, c=c, ci=ci, bk=bk, tl=tl, hblk=hblk: e.transpose(
                            bk[:, ci * 128:(ci + 1) * 128], hblk[:, c, tl * 128:(tl + 1) * 128], identf[:]),
                            reads=[hname, "identf"], writes=[bname])
                    dst = otile[:, cg * 512:(cg + 1) * 512]
                    if cg % 2 == 0:
                        p.op("act", lambda e, dst=dst, bk=bk: e.activation(dst, bk[:, :], AF.Copy),
                             reads=[bname], writes=[oname])
                    else:
                        p.op("dve", lambda e, dst=dst, bk=bk: e.tensor_copy(dst, bk[:, :]),
                             reads=[bname], writes=[oname])
                r0 = (b * 2 + tl) * 128
                p.dma("sp", self.out[r0:r0 + 128, :], otile, self.sem(oname), reads=[oname], writes=["out:%d" % r0])
        p.barrier()

    def phase_ffn_up(self, l):
        p = self.p
        cvt = self.cvt
        xn = self.xn_view()
        o = SCR_OFF
        yraw = [self.V(o + i * 8200, F32, [128, 2050]) for i in range(2)]
        o += 2 * 8200
        cc = [self.V(o + i * 8192, F32, [128, 2048]) for i in range(2)]
        o += 2 * 8192
        hmt = [self.V(o + i * 4096, BF16, [128, 2048]) for i in range(2)]
        for i in range(2):
            p.op("pool", lambda e, i=i: e.memset(yraw[i][:, 0:1], 0.0), writes=["yr%d" % i])
            p.op("pool", lambda e, i=i: e.memset(yraw[i][:, 2049:2050], 0.0), writes=["yr%d" % i])
        xn_names = ["xn:%d" % b for b in range(NBLK)]
        for s in range(22):
            slabs = [self.load_slab(self.w_up[l, s], 4096), self.load_slab(self.w_up[l, 22 + s], 4096)]
            for ci in range(2):
                j = 2 * s + ci
                for st in range(2):
                    sv, sname = slabs[st]
                    sv3 = sv.rearrange("p (k n) -> p k n", n=256)
                    ch = j + st * NCHF
                    w0 = cvt[:, CV_FCW + (l * 3 + 0) * 88 + ch:CV_FCW + (l * 3 + 0) * 88 + ch + 1]
                    w1 = cvt[:, CV_FCW + (l * 3 + 1) * 88 + ch:CV_FCW + (l * 3 + 1) * 88 + ch + 1]
                    w2 = cvt[:, CV_FCW + (l * 3 + 2) * 88 + ch:CV_FCW + (l * 3 + 2) * 88 + ch + 1]
                    bb = cvt[:, CV_FCB + l * 88 + ch:CV_FCB + l * 88 + ch + 1]
                    yr, yname = yraw[st], "yr%d" % st
                    cv_, cname = cc[st], "cc%d" % st
                    for tb in range(4):
                        bk, bname = self.bank()
                        for kc in range(16):
                            p.op("pe", lambda e, bk=bk, sv3=sv3, kc=kc, ci=ci, tb=tb: e.matmul(
                                bk[:, :], sv3[:, kc, ci * 128:(ci + 1) * 128], xn[:, kc, tb * 512:(tb + 1) * 512],
                                start=(kc == 0), stop=(kc == 15)),
                                reads=[sname] + xn_names[2 * tb:2 * tb + 2], writes=[bname])
                        p.op("act", lambda e, bk=bk, yr=yr, tb=tb: e.activation(
                            yr[:, 1 + tb * 512:1 + (tb + 1) * 512], bk[:, :], AF.Copy),
                            reads=[bname], writes=[yname])
                        p.op("act", lambda e, bk=bk, cv_=cv_, tb=tb, w1=w1, bb=bb: e.activation(
                            cv_[:, tb * 512:(tb + 1) * 512], bk[:, :], AF.Identity, bias=bb, scale=w1),
                            reads=[bname, "cvt"], writes=[cname])
                    p.op("dve", lambda e, yr=yr, cv_=cv_, w0=w0: e.scalar_tensor_tensor(
                        cv_[:, :], yr[:, 0:2048], w0, cv_[:, :], ALU.mult, ALU.add),
                        reads=[yname, cname, "cvt"], writes=[cname])
                    p.op("dve", lambda e, yr=yr, cv_=cv_, w2=w2: e.scalar_tensor_tensor(
                        cv_[:, :], yr[:, 2:2050], w2, cv_[:, :], ALU.mult, ALU.add),
                        reads=[yname, cname, "cvt"], writes=[cname])
                p.op("act", lambda e: e.activation(cc[0][:, :], cc[0][:, :], AF.Silu), reads=["cc0"], writes=["cc0"])
                ht, hname = hmt[j % 2], "hmt%d" % (j % 2)
                p.op("dve", lambda e, ht=ht: e.tensor_tensor(ht[:, :], cc[0][:, :], cc[1][:, :], ALU.mult),
                     reads=["cc0", "cc1"], writes=[hname])
                p.dma("sp", self.hm[j], ht, self.sem(hname), reads=[hname], writes=["hm:%d" % j])
        p.barrier()

    def phase_linear(self, slabs, KC, SW, src, nhalf):
        p = self.p
        TH = S // nhalf
        ntb = TH // 512
        inT = self.V(0, BF16, [128, KC, TH])
        NHB = 4
        hold = [self.V(SCR_OFF + i * 2048, F32, [128, 512]) for i in range(NHB)]
        nci = SW // 128
        for hf in range(nhalf):
            t0 = hf * TH
            in_names = []
            for k0 in range(0, KC, 4):
                k1 = min(KC, k0 + 4)
                nm = "inT:%d" % k0
                in_names.append(nm)
                p.dma("sp", inT[:, k0:k1, :], src[k0:k1, :, t0:t0 + TH].rearrange("c p t -> p c t"),
                      self.sem(nm), reads=[], writes=[nm])
            tiles = [(s, ci, tb) for s in range(len(slabs)) for ci in range(nci) for tb in range(ntb)]

            def tile_ap(t):
                s, ci, tb = t
                oc = s * nci + ci
                return self.hT[oc, :, t0 + tb * 512:t0 + (tb + 1) * 512], "hT:%d:%d" % (oc, (t0 // 512) + tb)

            def issue_load(i):
                ap, nm = tile_ap(tiles[i])
                p.dma("sp", hold[i % NHB], ap, self.sem("hold%d" % (i % NHB)), reads=[nm], writes=["hold%d" % (i % NHB)])

            PF = 2
            for i in range(min(PF, len(tiles))):
                issue_load(i)
            cur = None
            for i, (s, ci, tb) in enumerate(tiles):
                if i + PF < len(tiles):
                    issue_load(i + PF)
                if ci == 0 and tb == 0:
                    cur = self.load_slab(slabs[s], KC * SW)
                sv, sname = cur
                sv3 = sv.rearrange("p (k n) -> p k n", n=SW)
                bk, bname = self.bank()
                for kc in range(KC):
                    p.op("pe", lambda e, bk=bk, sv3=sv3, kc=kc, ci=ci, tb=tb: e.matmul(
                        bk[:, :], sv3[:, kc, ci * 128:(ci + 1) * 128], inT[:, kc, tb * 512:(tb + 1) * 512],
                        start=(kc == 0), stop=(kc == KC - 1)),
                        reads=[sname, in_names[kc // 4]], writes=[bname])
                hb, hn = hold[i % NHB], "hold%d" % (i % NHB)
                p.op("dve", lambda e, hb=hb, bk=bk: e.tensor_tensor(hb[:, :], bk[:, :], hb[:, :], ALU.add),
                     reads=[bname, hn], writes=[hn])
                ap, nm = tile_ap(tiles[i])
                p.dma("sp", ap, hb, self.sem(hn + "st"), reads=[hn], writes=[nm])
            p.barrier()

    def phase_sc(self):
        p = self.p
        cvt = self.cvt
        xn = self.xn_view()
        o = SCR_OFF
        hhs = self.V(o, F32, [128, 2048]); o += 8192
        pp = self.V(o, F32, [128, 2050]); o += 8200
        cc = self.V(o, F32, [128, 2048]); o += 8192
        zt = [self.V(o + i * 4096, BF16, [128, 2048]) for i in range(2)]
        p.op("pool", lambda e: e.memset(pp[:, 0:1], 0.0), writes=["pp"])
        p.op("pool", lambda e: e.memset(pp[:, 2049:2050], 0.0), writes=["pp"])
        xn_names = ["xn:%d" % b for b in range(NBLK)]
        for s in range(8):
            sl_b = self.load_slab(self.sc_in[s], 4096)
            sl_c = self.load_slab(self.sc_in[8 + s], 4096)
            sl_h = self.load_slab(self.sc_in[16 + s], 4096)
            for ci in range(2):
                j = 2 * s + ci
                w0 = cvt[:, CV_SCW + 0 * 16 + j:CV_SCW + 0 * 16 + j + 1]
                w1 = cvt[:, CV_SCW + 1 * 16 + j:CV_SCW + 1 * 16 + j + 1]
                w2 = cvt[:, CV_SCW + 2 * 16 + j:CV_SCW + 2 * 16 + j + 1]

                def proj(slab, tb):
                    sv, sname = slab
                    sv3 = sv.rearrange("p (k n) -> p k n", n=256)
                    bk, bname = self.bank()
                    for kc in range(16):
                        p.op("pe", lambda e, bk=bk, sv3=sv3, kc=kc, ci=ci, tb=tb: e.matmul(
                            bk[:, :], sv3[:, kc, ci * 128:(ci + 1) * 128], xn[:, kc, tb * 512:(tb + 1) * 512],
                            start=(kc == 0), stop=(kc == 15)),
                            reads=[sname] + xn_names[2 * tb:2 * tb + 2], writes=[bname])
                    return bk, bname

                for tb in range(4):
                    bk, bname = proj(sl_h, tb)
                    p.op("act", lambda e, bk=bk, tb=tb: e.activation(hhs[:, tb * 512:(tb + 1) * 512], bk[:, :], AF.Copy),
                         reads=[bname], writes=["hhs"])
                for tb in range(4):
                    bk, bname = proj(sl_c, tb)
                    p.op("dve", lambda e, bk=bk, tb=tb: e.tensor_tensor(
                        pp[:, 1 + tb * 512:1 + (tb + 1) * 512], bk[:, :], hhs[:, tb * 512:(tb + 1) * 512], ALU.mult),
                        reads=[bname, "hhs"], writes=["pp"])
                p.op("act", lambda e, w1=w1: e.activation(cc[:, :], pp[:, 1:2049], AF.Identity, scale=w1),
                     reads=["pp", "cvt"], writes=["cc"])
                p.op("dve", lambda e, w0=w0: e.scalar_tensor_tensor(cc[:, :], pp[:, 0:2048], w0, cc[:, :], ALU.mult, ALU.add),
                     reads=["pp", "cc", "cvt"], writes=["cc"])
                p.op("dve", lambda e, w2=w2: e.scalar_tensor_tensor(cc[:, :], pp[:, 2:2050], w2, cc[:, :], ALU.mult, ALU.add),
                     reads=["pp", "cc", "cvt"], writes=["cc"])
                z, zname = zt[j % 2], "zt%d" % (j % 2)
                for tb in range(4):
                    bk, bname = proj(sl_b, tb)
                    p.op("dve", lambda e, bk=bk, tb=tb, z=z: e.tensor_tensor(
                        z[:, tb * 512:(tb + 1) * 512], bk[:, :], cc[:, tb * 512:(tb + 1) * 512], ALU.mult),
                        reads=[bname, "cc"], writes=[zname])
                p.dma("sp", self.mo[j], z, self.sem(zname), reads=[zname], writes=["mo:%d" % j])
        p.barrier()

    def phase_gqa(self):
        p = self.p
        cvt, ones, rotb, epsb = self.cvt, self.onesb, self.rotb, self.epsb
        xn = self.xn_view()
        xn_names = ["xn:%d" % b for b in range(NBLK)]
        Vt = self.V(65536, BF16, [128, 16, 512])
        o = SCR_OFF
        cos = self.V(o, F32, [128, S]); o += 8192
        sin = self.V(o, F32, [128, S]); o += 8192
        KT = self.V(o, BF16, [128, 4, S]); o += 16384
        sqb = self.V(o, BF16, [128, 512]); o += 1024
        knb = self.V(o, BF16, [128, 512]); o += 1024
        sd = self.V(o, F32, [128, 512]); o += 2048
        rs = self.V(o, F32, [128, 512]); o += 2048
        t1 = self.V(o, F32, [128, 512]); o += 2048
        t2 = self.V(o, F32, [128, 512]); o += 2048
        qt = [self.V(o + i * 1024, BF16, [128, 512]) for i in range(2)]
        o += 2048
        p.dma("sp", cos, self.cs[0], self.sem("cos"), writes=["cos"])
        p.dma("sp", sin, self.cs[1], self.sem("sin"), writes=["sin"])

        for vs in range(2):
            sv, sname = self.load_slab(self.gq_qkv[10 + vs], 4096)
            sv3 = sv.rearrange("p (k n) -> p k n", n=256)
            for tc in range(16):
                bk, bname = self.bank()
                for kc in range(16):
                    p.op("pe", lambda e, bk=bk, sv3=sv3, kc=kc, tc=tc: e.matmul(
                        bk[:, 0:256], xn[:, kc, tc * 128:(tc + 1) * 128], sv3[:, kc, :],
                        start=(kc == 0), stop=(kc == 15)),
                        reads=[sname, xn_names[tc // 2]], writes=[bname])
                p.op("act", lambda e, bk=bk, tc=tc, vs=vs: e.activation(
                    Vt[:, tc, vs * 256:(vs + 1) * 256], bk[:, 0:256], AF.Copy),
                    reads=[bname], writes=["Vt"])

        def qk_tile(sv3, sname, ci, tb, gcol, dst, dname):
            bk, bname = self.bank()
            for kc in range(16):
                p.op("pe", lambda e, kc=kc: e.matmul(
                    bk[:, :], sv3[:, kc, ci * 128:(ci + 1) * 128], xn[:, kc, tb * 512:(tb + 1) * 512],
                    start=(kc == 0), stop=(kc == 15)),
                    reads=[sname] + xn_names[2 * tb:2 * tb + 2], writes=[bname])
            p.op("act", lambda e: e.activation(sqb, bk[:, :], AF.Square), reads=[bname], writes=["sqb"])
            b2, b2n = self.bank()
            p.op("pe", lambda e: e.matmul(b2[:, :], ones[:], sqb, start=True, stop=True),
                 reads=["sqb", "onesb"], writes=[b2n])
            p.op("act", lambda e: e.activation(sd, b2[:, :], AF.Sqrt, bias=epsb[:, 0:1], scale=1.0 / 128),
                 reads=[b2n, "epsb"], writes=["sd"])
            p.op("dve", lambda e: e.reciprocal(rs, sd), reads=["sd"], writes=["rs"])
            p.op("dve", lambda e: e.scalar_tensor_tensor(knb, bk[:, :], cvt[:, gcol:gcol + 1], rs, ALU.mult, ALU.mult),
                 reads=[bname, "rs", "cvt"], writes=["knb"])
            b3, b3n = self.bank()
            p.op("pe", lambda e: e.matmul(b3[:, :], rotb[:], knb, start=True, stop=True),
                 reads=["knb", "rotb"], writes=[b3n])
            p.op("pool", lambda e: e.tensor_tensor(t1, knb, cos[:, tb * 512:(tb + 1) * 512], ALU.mult),
                 reads=["knb", "cos"], writes=["t1"])
            p.op("dve", lambda e: e.tensor_tensor(t2, b3[:, :], sin[:, tb * 512:(tb + 1) * 512], ALU.mult),
                 reads=[b3n, "sin"], writes=["t2"])
            p.op("dve", lambda e: e.tensor_tensor(dst, t1, t2, ALU.add), reads=["t1", "t2"], writes=[dname])

        for ks in range(2):
            sv, sname = self.load_slab(self.gq_qkv[8 + ks], 4096)
            sv3 = sv.rearrange("p (k n) -> p k n", n=256)
            for ci in range(2):
                g = 2 * ks + ci
                for tb in range(4):
                    qk_tile(sv3, sname, ci, tb, CV_KN, KT[:, g, tb * 512:(tb + 1) * 512], "KT")
        k = 0
        for q_s in range(8):
            sv, sname = self.load_slab(self.gq_qkv[q_s], 4096)
            sv3 = sv.rearrange("p (k n) -> p k n", n=256)
            for ci in range(2):
                h = 2 * q_s + ci
                for tb in range(4):
                    qv, qn = qt[k % 2], "qt%d" % (k % 2)
                    k += 1
                    qk_tile(sv3, sname, ci, tb, CV_QN, qv, qn)
                    p.dma("sp", self.qs[h, :, tb * 512:(tb + 1) * 512], qv, self.sem(qn), reads=[qn],
                          writes=["qs:%d:%d" % (h, tb)])
        p.barrier()

        qg = [self.V(i * 16384, BF16, [128, 4, S]) for i in range(2)]
        PT = [self.V(32768 + i * 1024, BF16, [128, 512]) for i in range(4)]
        rc = [self.V(36864 + i * 2048, F32, [128, 512]) for i in range(2)]
        ot = [self.V(40960 + i * 1024, BF16, [128, 512]) for i in range(2)]
        scale = 128.0 ** -0.5
        def attend(g, qh, qb, it, qv, qn):
            h = 4 * g + qh
            bo, bon = self.banks[(it % 2) * 2], "ps%d" % ((it % 2) * 2)
            bs, bsn = self.banks[(it % 2) * 2 + 1], "ps%d" % ((it % 2) * 2 + 1)
            rcv, rcn = rc[it % 2], "rc%d" % (it % 2)
            otv, otn = ot[it % 2], "ot%d" % (it % 2)

            def s_step(kc):
                sb_i = 4 + (kc % 4)
                sbk, sbn = self.banks[sb_i], "ps%d" % sb_i
                ptv, ptn = PT[kc % 4], "PT%d" % (kc % 4)
                p.op("pe", lambda e: e.matmul(
                    sbk[:, :], KT[:, g, kc * 128:(kc + 1) * 128], qv[:, qh, qb * 512:(qb + 1) * 512],
                    start=True, stop=True), reads=["KT", qn], writes=[sbn])
                p.op("act", lambda e: e.activation(ptv, sbk[:, :], AF.Exp, scale=scale),
                     reads=[sbn], writes=[ptn])

            def pv_step(kc):
                ptv, ptn = PT[kc % 4], "PT%d" % (kc % 4)
                p.op("pe", lambda e: e.matmul(
                    bo[:, :], Vt[:, kc, g * 128:(g + 1) * 128], ptv, start=(kc == 0), stop=(kc == 15)),
                    reads=["Vt", ptn], writes=[bon])
                p.op("pe", lambda e: e.matmul(
                    bs[:, :], ones[:], ptv, start=(kc == 0), stop=(kc == 15)),
                    reads=["onesb", ptn], writes=[bsn])

            s_step(0)
            s_step(1)
            for kc in range(16):
                if kc + 2 < 16:
                    s_step(kc + 2)
                pv_step(kc)
            p.op("dve", lambda e: e.reciprocal(rcv, bs[:, :]), reads=[bsn], writes=[rcn])
            p.op("dve", lambda e: e.tensor_tensor(otv, bo[:, :], rcv, ALU.mult), reads=[bon, rcn], writes=[otn])
            p.dma("sp", self.mo[h, :, qb * 512:(qb + 1) * 512], otv, self.sem(otn), reads=[otn],
                  writes=["mo:%d:%d" % (h, qb)])

        it = 0
        for g in range(4):
            qv, qn = qg[g % 2], "qg%d" % (g % 2)
            p.dma("sp", qv, self.qs[4 * g:4 * g + 4].rearrange("c p t -> p c t"), self.sem(qn), reads=[], writes=[qn])
            for qh in range(4):
                for qb in range(4):
                    attend(g, qh, qb, it, qv, qn)
                    it += 1
        p.barrier()

    def phase_na(self, j):
        p = self.p
        ones, identb = self.onesb, self.identb
        xn = self.xn_view()
        xn_names = ["xn:%d" % b for b in range(NBLK)]
        Vt = [self.V(65536 + i * 8192, BF16, [128, 16, 256]) for i in range(2)]
        o = SCR_OFF
        qT = [self.V(o + i * 4096, BF16, [128, S]) for i in range(2)]; o += 8192
        kT = [self.V(o + i * 4096, BF16, [128, S]) for i in range(2)]; o += 8192
        bias = [self.V(o + i * 5376, BF16, [128, NA_NTAB * 128]) for i in range(2)]; o += 10752
        PT = [self.V(o + i * 1280, BF16, [128, 640]) for i in range(2)]; o += 2560
        rc = [self.V(o + i * 512, F32, [128, 128]) for i in range(2)]; o += 1024
        at = [self.V(o + i * 4096, BF16, [128, S]) for i in range(2)]; o += 8192
        scale = 128.0 ** -0.5
        banks = self.banks
        pj = [0]

        def proj_bank():
            i = pj[0] % 2
            pj[0] += 1
            return banks[i], "ps%d" % i

        def head(hp, ci, slq, slk, vt, vtn):
            h = 2 * hp + ci
            qv, qn = qT[h % 2], "qT%d" % (h % 2)
            kv, kn = kT[h % 2], "kT%d" % (h % 2)
            bv, bn = bias[h % 2], "bias%d" % (h % 2)
            av, an = at[h % 2], "at%d" % (h % 2)
            p.dma("pool", bv, self.na_bias[j, h], self.sem(bn), reads=[], writes=[bn])
            for (slab, dst, dn, isq) in ((slq, qv, qn, True), (slk, kv, kn, False)):
                sv, sname = slab
                sv3 = sv.rearrange("p (k n) -> p k n", n=256)
                for tb in range(4):
                    bk, bname = proj_bank()
                    for kc in range(16):
                        p.op("pe", lambda e, bk=bk, sv3=sv3, kc=kc, tb=tb: e.matmul(
                            bk[:, :], sv3[:, kc, ci * 128:(ci + 1) * 128], xn[:, kc, tb * 512:(tb + 1) * 512],
                            start=(kc == 0), stop=(kc == 15)),
                            reads=[sname] + xn_names[2 * tb:2 * tb + 2], writes=[bname])
                    if isq:
                        p.op("act", lambda e, bk=bk, dst=dst, tb=tb: e.activation(
                            dst[:, tb * 512:(tb + 1) * 512], bk[:, :], AF.Copy, scale=scale),
                            reads=[bname], writes=[dn])
                    else:
                        p.op("dve", lambda e, bk=bk, dst=dst, tb=tb: e.tensor_copy(
                            dst[:, tb * 512:(tb + 1) * 512], bk[:, :]), reads=[bname], writes=[dn])

            def qk(m):
                tbase, nch, kc0 = _na_pair(m)
                A, An = banks[2 + m % 2], "ps%d" % (2 + m % 2)
                B, Bn = banks[4 + m % 2], "ps%d" % (4 + m % 2)
                ptv, ptn = PT[m % 2], "PT%d" % (m % 2)
                for c in range(nch):
                    dst = A[:, c * 128:(c + 1) * 128] if c < 4 else B[:, 0:128]
                    dn = An if c < 4 else Bn
                    p.op("pe", lambda e, dst=dst, c=c: e.matmul(
                        dst, kv[:, (kc0 + c) * 128:(kc0 + c + 1) * 128], qv[:, m * 128:(m + 1) * 128],
                        start=True, stop=False), reads=[kn, qn], writes=[dn])
                    p.op("pe", lambda e, dst=dst, c=c: e.matmul(
                        dst, identb[:], bv[:, (tbase + c) * 128:(tbase + c + 1) * 128],
                        start=False, stop=True), reads=["identb", bn], writes=[dn])
                p.op("act", lambda e: e.activation(ptv[:, 0:512], A[:, :], AF.Exp), reads=[An], writes=[ptn])
                if nch == 5:
                    p.op("act", lambda e: e.activation(ptv[:, 512:640], B[:, 0:128], AF.Exp), reads=[Bn], writes=[ptn])

            def pv(m):
                tbase, nch, kc0 = _na_pair(m)
                O, On = banks[6 + m % 2], "ps%d" % (6 + m % 2)
                ptv, ptn = PT[m % 2], "PT%d" % (m % 2)
                rcv, rcn = rc[m % 2], "rc%d" % (m % 2)
                for c in range(nch):
                    p.op("pe", lambda e, c=c: e.matmul(
                        O[:, 0:128], vt[:, kc0 + c, ci * 128:(ci + 1) * 128], ptv[:, c * 128:(c + 1) * 128],
                        start=(c == 0), stop=(c == nch - 1)), reads=[vtn, ptn], writes=[On])
                for c in range(nch):
                    p.op("pe", lambda e, c=c: e.matmul(
                        O[:, 128:256], ones[:], ptv[:, c * 128:(c + 1) * 128],
                        start=(c == 0), stop=(c == nch - 1)), reads=["onesb", ptn], writes=[On])
                p.op("dve", lambda e: e.reciprocal(rcv, O[:, 128:256]), reads=[On], writes=[rcn])
                p.op("dve", lambda e: e.tensor_tensor(av[:, m * 128:(m + 1) * 128], O[:, 0:128], rcv, ALU.mult),
                     reads=[On, rcn], writes=[an])

            qk(0)
            for m in range(16):
                if m + 1 < 16:
                    qk(m + 1)
                pv(m)
            p.dma("sp", self.mo[h], av, self.sem(an), reads=[an], writes=["mo:%d" % h])

        for hp in range(8):
            slq = self.load_slab(self.na_qkv[j, hp], 4096)
            slk = self.load_slab(self.na_qkv[j, 8 + hp], 4096)
            slv = self.load_slab(self.na_qkv[j, 16 + hp], 4096)
            vt, vtn = Vt[hp % 2], "Vt%d" % (hp % 2)
            sv3 = slv[0].rearrange("p (k n) -> p k n", n=256)
            for tc in range(16):
                bk, bname = proj_bank()
                for kc in range(16):
                    p.op("pe", lambda e, bk=bk, sv3=sv3, kc=kc, tc=tc: e.matmul(
                        bk[:, 0:256], xn[:, kc, tc * 128:(tc + 1) * 128], sv3[:, kc, :],
                        start=(kc == 0), stop=(kc == 15)),
                        reads=[slv[1], xn_names[tc // 2]], writes=[bname])
                p.op("act", lambda e, bk=bk, tc=tc, vt=vt: e.activation(vt[:, tc, :], bk[:, 0:256], AF.Copy),
                     reads=[bname], writes=[vtn])
            for ci in range(2):
                head(hp, ci, slq, slk, vt, vtn)
        p.barrier()

    def build(self):
        cfg = self.cfg
        layers = cfg.get("layers", [0, 1, 2, 3])
        self.consts()
        first = True
        for l in layers:
            m, j = l % 3, l // 3
            if cfg.get("mixer", True):
                if first:
                    self.phase_input(CV_MIX + 16 * l)
                else:
                    self.phase_norm(CV_MIX + 16 * l)
                first = False
                if m == 0:
                    self.phase_na(j)
                    self.phase_linear([self.na_o[j, s] for s in range(8)], 16, 256, self.mo, 1)
                elif m == 1:
                    self.phase_sc()
                    self.phase_linear([self.sc_o[s] for s in range(8)], 16, 256, self.mo, 1)
                else:
                    self.phase_gqa()
                    self.phase_linear([self.gq_o[s] for s in range(8)], 16, 256, self.mo, 1)
            if cfg.get("ffn", True):
                if first:
                    self.phase_input(CV_FFN + 16 * l)
                else:
                    self.phase_norm(CV_FFN + 16 * l)
                first = False
                self.phase_ffn_up(l)
                self.phase_linear([self.w_dn[l, s] for s in range(16)], NCHF, 128, self.hm, 2)
        if first:
            self.phase_input(CV_FIN)
        if cfg.get("dump_h"):
            self.phase_dump()
        self.phase_output(CV_FIN)
        self.p.finish()
        self.es.close()
        return self.nc

    def phase_dump(self):
        p = self.p
        for c in range(16):
            p.dma("sp", self.hdump[c], self.hT[c], self.sem("dump%d" % c), reads=[], writes=["dump%d" % c])
        p.barrier()


def _tile_w(W, sw):
    K, N = W.shape
    kc = K // 128
    t = W.reshape(kc, 128, N // sw, sw).transpose(2, 1, 0, 3)
    return np.ascontiguousarray(t).reshape(N // sw, 128, kc * sw)


def _fm(v):
    v = np.asarray(v, np.float32)
    lead = v.shape[:-1]
    c = v.shape[-1] // 128
    t = v.reshape(*lead, c, 128)
    t = np.moveaxis(t, -1, 0)
    return np.ascontiguousarray(t).reshape(128, -1)


def _na_bias_tables(rpb):
    H = rpb.shape[0]
    out = np.empty((H, 128, NA_NTAB, 128), np.float32)
    pk = np.arange(128)[:, None]
    pq = np.arange(128)[None, :]
    specs = [((0, 1), 0, 4, 0), ((2, 3), 0, 4, 4), ((4, 5), 0, 5, 8), ((28, 29), 24, 4, 13), ((30, 31), 24, 4, 17)]
    for (r0, r1), a, nch, base in specs:
        r = np.where(pq < 64, r0, r1)
        qcol = pq % 64
        r_start = np.clip(r - 4, 0, 24)
        c_start = np.clip(qcol - 8, 0, 48)
        for c in range(nch):
            krow = a + 2 * c + pk // 64
            kcol = pk % 64
            valid = (krow >= r_start) & (krow < r_start + 8) & (kcol >= c_start) & (kcol < c_start + 16)
            dr = np.clip(krow - r + 7, 0, 14)
            dc = np.clip(kcol - qcol + 15, 0, 30)
            g = rpb[:, dr, dc]
            out[:, :, base + c, :] = np.where(valid[None], g, np.float32(-30000.0))
    return out.reshape(H, 128, NA_NTAB * 128)


def _rope_tables():
    t = np.arange(S)
    row = (t // 64).astype(np.float32)[:, None]
    col = (t % 64).astype(np.float32)[:, None]
    inv = (np.float32(10000.0) ** (-np.arange(0, 64, 2, dtype=np.float32) / np.float32(64))).astype(np.float32)
    ang = np.concatenate([row * inv, row * inv, col * inv, col * inv], axis=-1)
    return np.ascontiguousarray(np.stack([np.cos(ang).T, np.sin(ang).T]).astype(np.float32))


def _rot_matrix():
    R = np.zeros((128, 128), np.float32)
    for do in range(128):
        if (do % 64) < 32:
            R[do + 32, do] = -1.0
        else:
            R[do - 32, do] = 1.0
    return R


def prep_shared(inp):
    f = lambda a: np.asarray(a, np.float32)
    sh = {}
    cvp = [_fm(f(inp["mix_norm"])), _fm(f(inp["ffn_norm"])), _fm(f(inp["final_norm"])),
           _fm(f(inp["ffn_conv_w"])), _fm(f(inp["ffn_conv_b"])), _fm(f(inp["sc_conv_w"])[0]),
           f(inp["gqa_q_norm"])[0].reshape(128, 1), f(inp["gqa_k_norm"])[0].reshape(128, 1)]
    sh["cv"] = np.ascontiguousarray(np.concatenate(cvp, axis=1))
    assert sh["cv"].shape == (128, NCV), sh["cv"].shape
    sh["ident"] = np.eye(128, dtype=np.float32)
    sh["rot"] = _rot_matrix()
    sh["cs"] = _rope_tables()
    sh["w_up"] = np.stack([_tile_w(f(inp["ffn_w_up"])[l], 256) for l in range(4)])
    sh["w_dn"] = np.stack([_tile_w(f(inp["ffn_w_down"])[l], 128) for l in range(4)])
    sh["na_qkv"] = np.stack([_tile_w(f(inp["na_w_qkv"])[j], 256) for j in range(2)])
    sh["na_o"] = np.stack([_tile_w(f(inp["na_w_o"])[j], 256) for j in range(2)])
    sh["na_bias"] = np.stack([_na_bias_tables(f(inp["na_rpb"])[j]) for j in range(2)])
    sh["sc_in"] = _tile_w(f(inp["sc_w_in"])[0], 256)
    sh["sc_o"] = _tile_w(f(inp["sc_w_out"])[0], 256)
    sh["gq_qkv"] = _tile_w(f(inp["gqa_w_qkv"])[0], 256)
    sh["gq_o"] = _tile_w(f(inp["gqa_w_o"])[0], 256)
    return sh


_CFG = {"layers": [0, 1, 2, 3]}


def kernel(**inputs):
    x = np.asarray(inputs["x"], np.float32)
    sh = prep_shared(inputs)
    nc = Builder(dict(_CFG)).build()
    n = x.shape[0]
    in_maps = [dict(sh, x=np.ascontiguousarray(x[b])) for b in range(n)]
    res = run_bass_kernel_spmd(nc, in_maps, core_ids=list(range(n)))
    return np.stack([np.asarray(r["out"], np.float32) for r in res.results], axis=0)
```

```python
import numpy as np
import concourse.bass as bass
import concourse.mybir as mybir
from concourse.bass_utils import run_bass_kernel_spmd
from contextlib import ExitStack

F32 = mybir.dt.float32
BF16 = mybir.dt.bfloat16
AF = mybir.ActivationFunctionType
ALU = mybir.AluOpType


class _Op:
    __slots__ = ("eng", "fn", "deps", "needed", "sig", "is_dma")

    def __init__(self, eng, fn, deps, is_dma):
        self.eng = eng
        self.fn = fn
        self.deps = deps
        self.needed = False
        self.sig = None
        self.is_dma = is_dma


class _DSem:
    def __init__(self, sem):
        self.sem = sem
        self.count = 0


class Prog:
    ENG = ("pe", "act", "dve", "pool", "sp")
    SEM_LIMIT = 30000

    def __init__(self, nc, es):
        self.nc = nc
        self.es = es
        self.ops = {e: [] for e in self.ENG}
        self.res = {}
        self.esem = {e: es.enter_context(nc.semaphore("sem_" + e)) for e in self.ENG}
        self.nwaits = 0
        self.last_real = {}
        self.pending_dma = []

    def sbuf(self, name, shape, dt):
        return self.es.enter_context(self.nc.sbuf_tensor(name, shape, dt))

    def psum(self, name, shape, dt):
        return self.es.enter_context(self.nc.psum_tensor(name, shape, dt))

    def dsem(self, name):
        return _DSem(self.es.enter_context(self.nc.semaphore(name)))

    def _record(self, eng, fn, reads, writes, is_dma):
        deps = []
        for r in reads:
            st = self.res.get(r)
            if st is not None and st[0] is not None:
                deps.append(st[0])
        for w in writes:
            st = self.res.get(w)
            if st is not None:
                if st[0] is not None:
                    deps.append(st[0])
                deps.extend(st[1].values())
                deps.extend(st[2])
        op = _Op(eng, fn, deps, is_dma)
        for d in deps:
            if not (d.eng == "pe" and eng == "pe" and not d.is_dma and not is_dma):
                d.needed = True
        for r in reads:
            st = self.res.get(r)
            if st is None:
                st = self.res[r] = [None, {}, []]
            if is_dma:
                st[2].append(op)
            else:
                st[1][eng] = op
        for w in writes:
            self.res[w] = [op, {}, []]
        self.ops[eng].append(op)
        if fn is not None:
            if is_dma:
                self.pending_dma.append(op)
            else:
                self.last_real[eng] = op
        return op

    def op(self, eng, fn, reads=(), writes=()):
        return self._record(eng, fn, reads, writes, False)

    def dma(self, eng, out, in_, dsem, reads=(), writes=()):
        op = self._record(eng, lambda e: e.dma_start(out=out, in_=in_), reads, writes, True)
        if dsem.count + 16 > self.SEM_LIMIT:
            dsem.sem = self.es.enter_context(self.nc.semaphore())
            dsem.count = 0
        dsem.count += 16
        op.sig = (dsem.sem, dsem.count)
        return op

    def finish(self, final_waits=()):
        self.op("sp", None, reads=list(final_waits))
        for e in self.ENG:
            c = 0
            sem = self.esem[e]
            for op in self.ops[e]:
                if not op.is_dma and op.needed:
                    if c >= self.SEM_LIMIT:
                        sem = self.es.enter_context(self.nc.semaphore())
                        c = 0
                    c += 1
                    op.sig = (sem, c)
        engs = {"pe": "tensor", "act": "scalar", "dve": "vector", "pool": "gpsimd", "sp": "sync"}
        with self.nc.Block() as block:
            for e in self.ENG:
                getattr(block, engs[e])(lambda eng, e=e: self._emit(e, eng))

    def _emit(self, e, eng):
        waited = {}
        for op in self.ops[e]:
            for d in op.deps:
                if d.eng == "pe" and e == "pe" and not d.is_dma and not op.is_dma:
                    continue
                sem, val = d.sig
                key = id(sem)
                if waited.get(key, 0) >= val:
                    continue
                eng.wait_ge(sem, val)
                self.nwaits += 1
                waited[key] = val
            if op.fn is None:
                continue
            ins = op.fn(eng)
            if op.is_dma:
                ins.then_inc(op.sig[0], 16)
            elif op.needed:
                ins.then_inc(op.sig[0], 1)

    def barrier(self):
        lasts = [self.last_real[e] for e in self.ENG if self.last_real.get(e) is not None]
        lasts += self.pending_dma
        for e in self.ENG:
            op = _Op(e, None, list(lasts), False)
            self.ops[e].append(op)
        for d in lasts:
            d.needed = True
        self.pending_dma = []
        self.res = {}


S = 2048
D = 2048
DFF = 5632
NH = 16
EPS = 1e-6
NBLK = 8
BT = 256
NCHF = DFF // 128

CV_MIX = 0
CV_FFN = 64
CV_FIN = 128
CV_FCW = 144
CV_FCB = CV_FCW + 4 * 3 * 88
CV_SCW = CV_FCB + 4 * 88
CV_QN = CV_SCW + 48
CV_KN = CV_QN + 1
NCV = CV_KN + 1

BIG_B = 90112
RSLOT_B = 8192
NRING = 6
SCR_OFF = BIG_B + NRING * RSLOT_B
M_BYTES = 204800
RSTD_OFF = M_BYTES - 8192

NA_VAR = {0: (0, 4), 1: (4, 4), "int": (8, 5), 14: (13, 4), 15: (17, 4)}
NA_NTAB = 21


def _na_pair(m):
    if m in (0, 1):
        tb, nch = NA_VAR[m]
        return tb, nch, 0
    if m in (14, 15):
        tb, nch = NA_VAR[m]
        return tb, nch, 12
    tb, nch = NA_VAR["int"]
    return tb, nch, m - 2


class Builder:
    def __init__(self, cfg):
        self.cfg = cfg
        nc = self.nc = bass.Bass("TRN2", target_bir_lowering=False)
        self.es = ExitStack()
        p = self.p = Prog(nc, self.es)

        def din(name, shape, dt=F32):
            return nc.dram_tensor(name, shape, dt, kind="ExternalInput").ap()

        def dscr(name, shape, dt):
            return nc.dram_tensor(name, shape, dt, kind="Internal").ap()

        self.x = din("x", [S, D])
        self.cv = din("cv", [128, NCV])
        self.ident = din("ident", [128, 128])
        self.rot = din("rot", [128, 128])
        self.cs = din("cs", [2, 128, S])
        self.w_up = din("w_up", [4, 44, 128, 4096])
        self.w_dn = din("w_dn", [4, 16, 2, 128, 2816])
        self.na_qkv = din("na_qkv", [2, 24, 128, 4096])
        self.na_o = din("na_o", [2, 8, 128, 4096])
        self.na_bias = din("na_bias", [2, NH, 128, NA_NTAB * 128])
        self.sc_in = din("sc_in", [24, 128, 4096])
        self.sc_o = din("sc_o", [8, 128, 4096])
        self.gq_qkv = din("gq_qkv", [12, 128, 4096])
        self.gq_o = din("gq_o", [8, 128, 4096])
        self.out = nc.dram_tensor("out", [S, D], F32, kind="ExternalOutput").ap()
        self.hT = dscr("hT", [16, 128, S], F32)
        self.hm = dscr("hm", [2, 128, NCHF, S // 2], BF16)
        self.mo = dscr("mo", [128, 16, S], BF16)
        self.qs = dscr("qs", [128, 16, S], BF16)
        if cfg.get("dump_h"):
            self.hdump = nc.dram_tensor("hdump", [16, 128, S], F32, kind="ExternalOutput").ap()

        self.M = p.sbuf("M", [128, M_BYTES // 4], F32)
        self.cvt = p.sbuf("cvt", [128, NCV], F32)
        self.identf = p.sbuf("identf", [128, 128], F32)
        self.identb = p.sbuf("identb", [128, 128], BF16)
        self.onesb = p.sbuf("onesb", [128, 128], BF16)
        self.rotb = p.sbuf("rotb", [128, 128], BF16)
        self.epsb = p.sbuf("epsb", [128, 1], F32)
        self.banks = [p.psum("ps%d" % i, [128, 512], F32) for i in range(8)]
        self.bank_i = 0
        self.bank_pool = list(range(8))
        self.sems = {}
        self.ring_i = 0

    def V(self, off, dt, shape):
        n = 1
        for s_ in shape[1:]:
            n *= s_
        esz = 4 if dt == F32 else 2
        assert off % 4 == 0 and (n * esz) % 4 == 0
        assert off + n * esz <= M_BYTES, (off, n, esz)
        ap = self.M[:, off // 4:(off + n * esz) // 4]
        if dt != F32:
            ap = ap.bitcast(dt)
        if len(shape) == 3:
            ap = ap.rearrange("p (a b) -> p a b", b=shape[2])
        return ap

    def sem(self, name):
        if name not in self.sems:
            self.sems[name] = self.p.dsem("d_" + name.replace(":", "_"))
        return self.sems[name]

    def bank(self):
        i = self.bank_pool[self.bank_i % len(self.bank_pool)]
        self.bank_i += 1
        return self.banks[i], "ps%d" % i

    def ring_slot(self):
        i = self.ring_i
        self.ring_i = (i + 1) % NRING
        return i

    def load_slab(self, src, nel):
        i = self.ring_slot()
        name = "ring%d" % i
        v = self.V(BIG_B + i * RSLOT_B, BF16, [128, nel])
        self.p.dma("pool", v, src, self.sem(name), reads=[], writes=[name])
        return v, name

    def consts(self):
        p = self.p
        p.dma("sp", self.cvt[:], self.cv, self.sem("cvt"), writes=["cvt"])
        p.dma("sp", self.identf[:], self.ident, self.sem("identf"), writes=["identf"])
        p.dma("pool", self.identb[:], self.ident, self.sem("identb"), writes=["identb"])
        p.dma("pool", self.rotb[:], self.rot, self.sem("rotb"), writes=["rotb"])
        p.op("pool", lambda e: e.memset(self.onesb[:], 1.0), writes=["onesb"])
        p.op("pool", lambda e: e.memset(self.epsb[:], EPS), writes=["epsb"])
        p.barrier()

    def rstd_view(self):
        return self.V(RSTD_OFF, F32, [128, S])

    def norm_apply(self, hblk, hname, b, gcol, out_ap, out_name):
        p = self.p
        cvt = self.cvt
        rsv = self.rstd_view()[:, b * BT:(b + 1) * BT]
        for c in range(16):
            p.op("dve", lambda e, c=c: e.scalar_tensor_tensor(
                out_ap[:, c, :], hblk[:, c, :], cvt[:, gcol + c:gcol + c + 1], rsv, ALU.mult, ALU.mult),
                reads=[hname, "rstd:%d" % (b // 2), "cvt"], writes=[out_name])

    def norm_stats(self, hblk, hname, b, sqv, sdv):
        p = self.p
        ones, epsb = self.onesb, self.epsb
        rsv = self.rstd_view()[:, b * BT:(b + 1) * BT]
        p.op("act", lambda e: e.activation(sqv, hblk, AF.Square), reads=[hname], writes=["sq"])
        bk, bname = self.bank()
        for c in range(16):
            p.op("pe", lambda e, c=c: e.matmul(bk[:, 0:BT], ones[:], sqv[:, c, :], start=(c == 0), stop=(c == 15)),
                 reads=["sq", "onesb"], writes=[bname])
        p.op("act", lambda e: e.activation(sdv, bk[:, 0:BT], AF.Sqrt, bias=epsb[:, 0:1], scale=1.0 / D),
             reads=[bname, "epsb"], writes=["sd"])
        p.op("dve", lambda e: e.reciprocal(rsv, sdv), reads=["sd"], writes=["rstd:%d" % (b // 2)])

    def xn_view(self):
        return self.V(0, BF16, [128, 16, S])

    def phase_input(self, gcol):
        p = self.p
        o = SCR_OFF
        hb = [self.V(o + i * 16384, F32, [128, 16, BT]) for i in range(2)]
        o += 32768
        xt = self.V(o, F32, [128, 2048]); o += 8192
        sq = self.V(o, BF16, [128, 16, BT]); o += 8192
        sd = self.V(o, F32, [128, BT]); o += 1024
        xn = self.xn_view()
        identf = self.identf
        for b in range(NBLK):
            hblk, hname = hb[b % 2], "hb%d" % (b % 2)
            for tl in range(2):
                r0 = (b * 2 + tl) * 128
                p.dma("sp", xt, self.x[r0:r0 + 128, :], self.sem("xt"), writes=["xt"])
                for cg in range(4):
                    bk, bname = self.bank()
                    for ci in range(4):
                        c = cg * 4 + ci
                        p.op("pe", lambda e, c=c, ci=ci, bk=bk: e.transpose(
                            bk[:, ci * 128:(ci + 1) * 128], xt[:, c * 128:(c + 1) * 128], identf[:]),
                            reads=["xt", "identf"], writes=[bname])
                    dst = hblk[:, cg * 4:(cg + 1) * 4, tl * 128:(tl + 1) * 128]
                    src = bk[:, :].rearrange("p (a b) -> p a b", b=128)
                    if cg % 2 == 0:
                        p.op("act", lambda e, dst=dst, src=src: e.activation(dst, src, AF.Copy),
                             reads=[bname], writes=[hname])
                    else:
                        p.op("dve", lambda e, dst=dst, src=src: e.tensor_copy(dst, src),
                             reads=[bname], writes=[hname])
            p.dma("sp", self.hT[:, :, b * BT:(b + 1) * BT].rearrange("c p t -> p c t"), hblk,
                  self.sem(hname + "st"), reads=[hname], writes=["hT:b%d" % b])
            self.norm_stats(hblk, hname, b, sq, sd)
            self.norm_apply(hblk, hname, b, gcol, xn[:, :, b * BT:(b + 1) * BT], "xn:%d" % b)
        p.barrier()

    def phase_norm(self, gcol):
        p = self.p
        NB = 3
        hb = [self.V(SCR_OFF + i * 16384, F32, [128, 16, BT]) for i in range(NB)]
        xn = self.xn_view()
        for b in range(NBLK):
            hblk, hname = hb[b % NB], "hb%d" % (b % NB)
            p.dma("sp", hblk, self.hT[:, :, b * BT:(b + 1) * BT].rearrange("c p t -> p c t"),
                  self.sem(hname), reads=[], writes=[hname])
            self.norm_apply(hblk, hname, b, gcol, xn[:, :, b * BT:(b + 1) * BT], "xn:%d" % b)
        p.barrier()

    def phase_output(self, gcol):
        p = self.p
        NB = 3
        hb = [self.V(SCR_OFF + i * 16384, F32, [128, 16, BT]) for i in range(NB)]
        identf = self.identf
        ot = [self.V(0 + i * 8192, F32, [128, 2048]) for i in range(2)]
        k = 0
        for b in range(NBLK):
            hblk, hname = hb[b % NB], "hb%d" % (b % NB)
            p.dma("sp", hblk, self.hT[:, :, b * BT:(b + 1) * BT].rearrange("c p t -> p c t"),
                  self.sem(hname), reads=[], writes=[hname])
            self.norm_apply(hblk, hname, b, gcol, hblk, hname)
            for tl in range(2):
                otile, oname = ot[k % 2], "ot%d" % (k % 2)
                k += 1
                for cg in range(4):
                    bk, bname = self.bank()
                    for ci in range(4):
                        c = cg * 4 + ci
                        p.op("pe", lambda e, c=c, ci=ci, bk=bk, tl=tl, hblk=hblk: e.transpose(
                            bk[:, ci * 128:(ci + 1) * 128], hblk[:, c, tl * 128:(tl + 1) * 128], identf[:]),
                            reads=[hname, "identf"], writes=[bname])
                    dst = otile[:, cg * 512:(cg + 1) * 512]
                    if cg % 2 == 0:
                        p.op("act", lambda e, dst=dst, bk=bk: e.activation(dst, bk[:, :], AF.Copy),
                             reads=[bname], writes=[oname])
                    else:
                        p.op("dve", lambda e, dst=dst, bk=bk: e.tensor_copy(dst, bk[:, :]),
                             reads=[bname], writes=[oname])
                r0 = (b * 2 + tl) * 128
                p.dma("sp", self.out[r0:r0 + 128, :], otile, self.sem(oname), reads=[oname], writes=["out:%d" % r0])
        p.barrier()

    def phase_ffn_up(self, l):
        p = self.p
        cvt = self.cvt
        xn = self.xn_view()
        o = SCR_OFF
        yraw = [self.V(o + i * 8200, F32, [128, 2050]) for i in range(2)]
        o += 2 * 8200
        cc = [self.V(o + i * 8192, F32, [128, 2048]) for i in range(2)]
        o += 2 * 8192
        hmt = [self.V(o + i * 4096, BF16, [128, 2048]) for i in range(2)]
        for i in range(2):
            p.op("pool", lambda e, i=i: e.memset(yraw[i][:, 0:1], 0.0), writes=["yr%d" % i])
            p.op("pool", lambda e, i=i: e.memset(yraw[i][:, 2049:2050], 0.0), writes=["yr%d" % i])
        xn_names = ["xn:%d" % b for b in range(NBLK)]
        for s in range(22):
            slabs = [self.load_slab(self.w_up[l, s], 4096), self.load_slab(self.w_up[l, 22 + s], 4096)]
            for ci in range(2):
                j = 2 * s + ci
                for st in range(2):
                    sv, sname = slabs[st]
                    sv3 = sv.rearrange("p (k n) -> p k n", n=256)
                    ch = j + st * NCHF
                    w0 = cvt[:, CV_FCW + (l * 3 + 0) * 88 + ch:CV_FCW + (l * 3 + 0) * 88 + ch + 1]
                    w1 = cvt[:, CV_FCW + (l * 3 + 1) * 88 + ch:CV_FCW + (l * 3 + 1) * 88 + ch + 1]
                    w2 = cvt[:, CV_FCW + (l * 3 + 2) * 88 + ch:CV_FCW + (l * 3 + 2) * 88 + ch + 1]
                    bb = cvt[:, CV_FCB + l * 88 + ch:CV_FCB + l * 88 + ch + 1]
                    yr, yname = yraw[st], "yr%d" % st
                    cv_, cname = cc[st], "cc%d" % st
                    for tb in range(4):
                        bk, bname = self.bank()
                        for kc in range(16):
                            p.op("pe", lambda e, bk=bk, sv3=sv3, kc=kc, ci=ci, tb=tb: e.matmul(
                                bk[:, :], sv3[:, kc, ci * 128:(ci + 1) * 128], xn[:, kc, tb * 512:(tb + 1) * 512],
                                start=(kc == 0), stop=(kc == 15)),
                                reads=[sname] + xn_names[2 * tb:2 * tb + 2], writes=[bname])
                        p.op("act", lambda e, bk=bk, yr=yr, tb=tb: e.activation(
                            yr[:, 1 + tb * 512:1 + (tb + 1) * 512], bk[:, :], AF.Copy),
                            reads=[bname], writes=[yname])
                        p.op("act", lambda e, bk=bk, cv_=cv_, tb=tb, w1=w1, bb=bb: e.activation(
                            cv_[:, tb * 512:(tb + 1) * 512], bk[:, :], AF.Identity, bias=bb, scale=w1),
                            reads=[bname, "cvt"], writes=[cname])
                    p.op("dve", lambda e, yr=yr, cv_=cv_, w0=w0: e.scalar_tensor_tensor(
                        cv_[:, :], yr[:, 0:2048], w0, cv_[:, :], ALU.mult, ALU.add),
                        reads=[yname, cname, "cvt"], writes=[cname])
                    p.op("dve", lambda e, yr=yr, cv_=cv_, w2=w2: e.scalar_tensor_tensor(
                        cv_[:, :], yr[:, 2:2050], w2, cv_[:, :], ALU.mult, ALU.add),
                        reads=[yname, cname, "cvt"], writes=[cname])
                p.op("act", lambda e: e.activation(cc[0][:, :], cc[0][:, :], AF.Silu), reads=["cc0"], writes=["cc0"])
                ht, hname = hmt[j % 2], "hmt%d" % (j % 2)
                p.op("dve", lambda e, ht=ht: e.tensor_tensor(ht[:, :], cc[0][:, :], cc[1][:, :], ALU.mult),
                     reads=["cc0", "cc1"], writes=[hname])
                for hf in range(2):
                    p.dma("sp", self.hm[hf, :, j, :], ht[:, hf * 1024:(hf + 1) * 1024], self.sem(hname + "h%d" % hf),
                          reads=[hname], writes=["hm:%d:%d" % (j, hf)])
        p.barrier()

    def phase_linear(self, groups, KC, SW, src_fn, nhalf):
        p = self.p
        ones, epsb = self.onesb, self.epsb
        TH = S // nhalf
        ntb = TH // 512
        inT = self.V(0, BF16, [128, KC, TH])
        NHB = 4
        o = SCR_OFF
        hold = [self.V(o + i * 2048, F32, [128, 512]) for i in range(NHB)]
        o += NHB * 2048
        sqt = [self.V(o + i * 1024, BF16, [128, 512]) for i in range(NHB)]
        o += NHB * 1024
        sdv = self.V(o, F32, [128, 512]); o += 2048
        rstd = self.rstd_view()
        nci = SW // 128
        ngr = len(groups)
        for hf in range(nhalf):
            t0 = hf * TH
            ssq_banks = [7 - tb for tb in range(ntb)]
            self.bank_pool = [i for i in range(8) if i not in ssq_banks]
            in_names = []
            for k0 in range(0, KC, 4):
                k1 = min(KC, k0 + 4)
                nm = "inT:%d" % k0
                in_names.append(nm)
                p.dma("sp", inT[:, k0:k1, :], src_fn(hf, k0, k1), self.sem(nm), reads=[], writes=[nm])
            tiles = [(s_, ci, tb) for s_ in range(ngr) for ci in range(nci) for tb in range(ntb)]

            def tile_ap(t):
                s_, ci, tb = t
                oc = s_ * nci + ci
                return self.hT[oc, :, t0 + tb * 512:t0 + (tb + 1) * 512], "hT:%d:%d" % (oc, (t0 // 512) + tb)

            def issue_load(i):
                ap, nm = tile_ap(tiles[i])
                p.dma("sp", hold[i % NHB], ap, self.sem("hold%d" % (i % NHB)), reads=[nm], writes=["hold%d" % (i % NHB)])

            def ssq_mm(i):
                s_, ci, tb = tiles[i]
                oc = s_ * nci + ci
                bi = ssq_banks[tb]
                p.op("pe", lambda e: e.matmul(self.banks[bi][:, :], ones[:], sqt[i % NHB][:, :],
                                              start=(oc == 0), stop=(oc == ngr * nci - 1)),
                     reads=["sqt%d" % (i % NHB), "onesb"], writes=["ps%d" % bi])

            PF = 2
            for i in range(min(PF, len(tiles))):
                issue_load(i)
            cur = None
            for i, (s_, ci, tb) in enumerate(tiles):
                if i + PF < len(tiles):
                    issue_load(i + PF)
                if ci == 0 and tb == 0:
                    cur = []
                    for (src, nel, kc0, nkc) in groups[s_]:
                        sv, sname = self.load_slab(src, nel)
                        cur.append((sv.rearrange("p (k n) -> p k n", n=SW), sname, kc0, nkc))
                bk, bname = self.bank()
                for (sv3, sname, kc0, nkc) in cur:
                    for kk in range(nkc):
                        kc = kc0 + kk
                        p.op("pe", lambda e, bk=bk, sv3=sv3, kk=kk, kc=kc, ci=ci, tb=tb: e.matmul(
                            bk[:, :], sv3[:, kk, ci * 128:(ci + 1) * 128], inT[:, kc, tb * 512:(tb + 1) * 512],
                            start=(kc == 0), stop=(kc == KC - 1)),
                            reads=[sname, in_names[kc // 4]], writes=[bname])
                if i >= 2:
                    ssq_mm(i - 2)
                hb, hn = hold[i % NHB], "hold%d" % (i % NHB)
                p.op("dve", lambda e, hb=hb, bk=bk: e.tensor_tensor(hb[:, :], bk[:, :], hb[:, :], ALU.add),
                     reads=[bname, hn], writes=[hn])
                p.op("act", lambda e, hb=hb, i=i: e.activation(sqt[i % NHB][:, :], hb[:, :], AF.Square),
                     reads=[hn], writes=["sqt%d" % (i % NHB)])
                ap, nm = tile_ap(tiles[i])
                p.dma("sp", ap, hb, self.sem(hn + "st"), reads=[hn], writes=[nm])
            for i in range(max(0, len(tiles) - 2), len(tiles)):
                ssq_mm(i)
            for tb in range(ntb):
                bi = ssq_banks[tb]
                tk = (t0 // 512) + tb
                p.op("act", lambda e, bi=bi: e.activation(sdv, self.banks[bi][:, :], AF.Sqrt, bias=epsb[:, 0:1], scale=1.0 / D),
                     reads=["ps%d" % bi, "epsb"], writes=["sdl"])
                p.op("dve", lambda e, tk=tk: e.reciprocal(rstd[:, tk * 512:(tk + 1) * 512], sdv),
                     reads=["sdl"], writes=["rstd:%d" % tk])
            self.bank_pool = list(range(8))
            p.barrier()

    def phase_sc(self):
        p = self.p
        cvt = self.cvt
        xn = self.xn_view()
        o = SCR_OFF
        hhs = self.V(o, F32, [128, 2048]); o += 8192
        pp = self.V(o, F32, [128, 2050]); o += 8200
        cc = self.V(o, F32, [128, 2048]); o += 8192
        zt = [self.V(o + i * 4096, BF16, [128, 2048]) for i in range(2)]
        p.op("pool", lambda e: e.memset(pp[:, 0:1], 0.0), writes=["pp"])
        p.op("pool", lambda e: e.memset(pp[:, 2049:2050], 0.0), writes=["pp"])
        xn_names = ["xn:%d" % b for b in range(NBLK)]
        for s in range(8):
            sl_b = self.load_slab(self.sc_in[s], 4096)
            sl_c = self.load_slab(self.sc_in[8 + s], 4096)
            sl_h = self.load_slab(self.sc_in[16 + s], 4096)
            for ci in range(2):
                j = 2 * s + ci
                w0 = cvt[:, CV_SCW + 0 * 16 + j:CV_SCW + 0 * 16 + j + 1]
                w1 = cvt[:, CV_SCW + 1 * 16 + j:CV_SCW + 1 * 16 + j + 1]
                w2 = cvt[:, CV_SCW + 2 * 16 + j:CV_SCW + 2 * 16 + j + 1]

                def proj(slab, tb):
                    sv, sname = slab
                    sv3 = sv.rearrange("p (k n) -> p k n", n=256)
                    bk, bname = self.bank()
                    for kc in range(16):
                        p.op("pe", lambda e, bk=bk, sv3=sv3, kc=kc, ci=ci, tb=tb: e.matmul(
                            bk[:, :], sv3[:, kc, ci * 128:(ci + 1) * 128], xn[:, kc, tb * 512:(tb + 1) * 512],
                            start=(kc == 0), stop=(kc == 15)),
                            reads=[sname] + xn_names[2 * tb:2 * tb + 2], writes=[bname])
                    return bk, bname

                for tb in range(4):
                    bk, bname = proj(sl_h, tb)
                    p.op("act", lambda e, bk=bk, tb=tb: e.activation(hhs[:, tb * 512:(tb + 1) * 512], bk[:, :], AF.Copy),
                         reads=[bname], writes=["hhs"])
                for tb in range(4):
                    bk, bname = proj(sl_c, tb)
                    p.op("dve", lambda e, bk=bk, tb=tb: e.tensor_tensor(
                        pp[:, 1 + tb * 512:1 + (tb + 1) * 512], bk[:, :], hhs[:, tb * 512:(tb + 1) * 512], ALU.mult),
                        reads=[bname, "hhs"], writes=["pp"])
                p.op("act", lambda e, w1=w1: e.activation(cc[:, :], pp[:, 1:2049], AF.Identity, scale=w1),
                     reads=["pp", "cvt"], writes=["cc"])
                p.op("dve", lambda e, w0=w0: e.scalar_tensor_tensor(cc[:, :], pp[:, 0:2048], w0, cc[:, :], ALU.mult, ALU.add),
                     reads=["pp", "cc", "cvt"], writes=["cc"])
                p.op("dve", lambda e, w2=w2: e.scalar_tensor_tensor(cc[:, :], pp[:, 2:2050], w2, cc[:, :], ALU.mult, ALU.add),
                     reads=["pp", "cc", "cvt"], writes=["cc"])
                z, zname = zt[j % 2], "zt%d" % (j % 2)
                for tb in range(4):
                    bk, bname = proj(sl_b, tb)
                    p.op("dve", lambda e, bk=bk, tb=tb, z=z: e.tensor_tensor(
                        z[:, tb * 512:(tb + 1) * 512], bk[:, :], cc[:, tb * 512:(tb + 1) * 512], ALU.mult),
                        reads=[bname, "cc"], writes=[zname])
                p.dma("sp", self.mo[:, j, :], z, self.sem(zname), reads=[zname], writes=["mo:%d" % j])
        p.barrier()

    def phase_gqa(self):
        p = self.p
        cvt, ones, rotb, epsb = self.cvt, self.onesb, self.rotb, self.epsb
        xn = self.xn_view()
        xn_names = ["xn:%d" % b for b in range(NBLK)]
        Vt = self.V(65536, BF16, [128, 16, 512])
        o = SCR_OFF
        cos = self.V(o, F32, [128, S]); o += 8192
        sin = self.V(o, F32, [128, S]); o += 8192
        KT = self.V(o, BF16, [128, 4, S]); o += 16384
        tmps = []
        for i in range(3):
            d = {"i": i}
            d["sqb"] = self.V(o, BF16, [128, 512]); o += 1024
            d["knb"] = self.V(o, BF16, [128, 512]); o += 1024
            d["sd"] = self.V(o, F32, [128, 512]); o += 2048
            d["rs"] = self.V(o, F32, [128, 512]); o += 2048
            d["t1"] = self.V(o, F32, [128, 512]); o += 2048
            d["t2"] = self.V(o, F32, [128, 512]); o += 2048
            tmps.append(d)
        qt = [self.V(81920 + i * 1024, BF16, [128, 512]) for i in range(3)]
        assert o <= M_BYTES
        p.dma("sp", cos, self.cs[0], self.sem("cos"), writes=["cos"])
        p.dma("sp", sin, self.cs[1], self.sem("sin"), writes=["sin"])

        for vs in range(2):
            sv, sname = self.load_slab(self.gq_qkv[10 + vs], 4096)
            sv3 = sv.rearrange("p (k n) -> p k n", n=256)
            for tc in range(16):
                bk, bname = self.bank()
                for kc in range(16):
                    p.op("pe", lambda e, bk=bk, sv3=sv3, kc=kc, tc=tc: e.matmul(
                        bk[:, 0:256], xn[:, kc, tc * 128:(tc + 1) * 128], sv3[:, kc, :],
                        start=(kc == 0), stop=(kc == 15)),
                        reads=[sname, xn_names[tc // 2]], writes=[bname])
                p.op("act", lambda e, bk=bk, tc=tc, vs=vs: e.activation(
                    Vt[:, tc, vs * 256:(vs + 1) * 256], bk[:, 0:256], AF.Copy),
                    reads=[bname], writes=["Vt"])

        tiles = []
        for ks in range(2):
            for ci in range(2):
                for tb in range(4):
                    tiles.append(dict(slab=8 + ks, ci=ci, tb=tb, gcol=CV_KN,
                                      dst=KT[:, 2 * ks + ci, tb * 512:(tb + 1) * 512], dname="KT", store=None))
        nq = 0
        for q_s in range(8):
            for ci in range(2):
                for tb in range(4):
                    tiles.append(dict(slab=q_s, ci=ci, tb=tb, gcol=CV_QN, dst=qt[nq % 3], dname="qt%d" % (nq % 3),
                                      store=(2 * q_s + ci, tb)))
                    nq += 1
        slabs = {}

        def stage_a(t, i):
            if t["slab"] not in slabs:
                sv, sname = self.load_slab(self.gq_qkv[t["slab"]], 4096)
                slabs[t["slab"]] = (sv.rearrange("p (k n) -> p k n", n=256), sname)
            sv3, sname = slabs[t["slab"]]
            d = t["d"] = tmps[i % 3]
            ti = d["i"]
            ci, tb = t["ci"], t["tb"]
            bk, bname = self.bank()
            t["bk"], t["bname"] = bk, bname
            for kc in range(16):
                p.op("pe", lambda e, kc=kc: e.matmul(
                    bk[:, :], sv3[:, kc, ci * 128:(ci + 1) * 128], xn[:, kc, tb * 512:(tb + 1) * 512],
                    start=(kc == 0), stop=(kc == 15)),
                    reads=[sname] + xn_names[2 * tb:2 * tb + 2], writes=[bname])
            p.op("act", lambda e: e.activation(d["sqb"], bk[:, :], AF.Square), reads=[bname], writes=["sqb%d" % ti])

        def stage_b(t):
            d = t["d"]
            ti = d["i"]
            bk, bname, gcol = t["bk"], t["bname"], t["gcol"]
            b2, b2n = self.bank()
            p.op("pe", lambda e: e.matmul(b2[:, :], ones[:], d["sqb"], start=True, stop=True),
                 reads=["sqb%d" % ti, "onesb"], writes=[b2n])
            p.op("act", lambda e: e.activation(d["sd"], b2[:, :], AF.Sqrt, bias=epsb[:, 0:1], scale=1.0 / 128),
                 reads=[b2n, "epsb"], writes=["sd%d" % ti])
            p.op("dve", lambda e: e.reciprocal(d["rs"], d["sd"]), reads=["sd%d" % ti], writes=["rs%d" % ti])
            p.op("dve", lambda e: e.scalar_tensor_tensor(d["knb"], bk[:, :], cvt[:, gcol:gcol + 1], d["rs"],
                                                         ALU.mult, ALU.mult),
                 reads=[bname, "rs%d" % ti, "cvt"], writes=["knb%d" % ti])

        def stage_c(t):
            d = t["d"]
            ti = d["i"]
            tb, dst, dname = t["tb"], t["dst"], t["dname"]
            b3, b3n = self.bank()
            p.op("pe", lambda e: e.matmul(b3[:, :], rotb[:], d["knb"], start=True, stop=True),
                 reads=["knb%d" % ti, "rotb"], writes=[b3n])
            p.op("pool", lambda e: e.tensor_tensor(d["t1"], d["knb"], cos[:, tb * 512:(tb + 1) * 512], ALU.mult),
                 reads=["knb%d" % ti, "cos"], writes=["t1%d" % ti])
            p.op("dve", lambda e: e.tensor_tensor(d["t2"], b3[:, :], sin[:, tb * 512:(tb + 1) * 512], ALU.mult),
                 reads=[b3n, "sin"], writes=["t2%d" % ti])
            p.op("dve", lambda e: e.tensor_tensor(dst, d["t1"], d["t2"], ALU.add),
                 reads=["t1%d" % ti, "t2%d" % ti], writes=[dname])
            if t["store"] is not None:
                h, tb_ = t["store"]
                p.dma("sp", self.qs[:, h, tb_ * 512:(tb_ + 1) * 512], dst, self.sem(dname), reads=[dname],
                      writes=["qs:%d:%d" % (h, tb_)])

        nt = len(tiles)
        for i in range(nt + 2):
            if i < nt:
                stage_a(tiles[i], i)
            if 0 <= i - 1 < nt:
                stage_b(tiles[i - 1])
            if 0 <= i - 2 < nt:
                stage_c(tiles[i - 2])
        p.barrier()

        qg = [self.V(i * 16384, BF16, [128, 4, S]) for i in range(2)]
        PT = [self.V(32768 + i * 1024, BF16, [128, 512]) for i in range(4)]
        rc = [self.V(36864 + i * 2048, F32, [128, 512]) for i in range(2)]
        ot = [self.V(40960 + i * 1024, BF16, [128, 512]) for i in range(2)]
        scale = 128.0 ** -0.5
        def attend(g, qh, qb, it, qv, qn):
            h = 4 * g + qh
            bo, bon = self.banks[(it % 2) * 2], "ps%d" % ((it % 2) * 2)
            bs, bsn = self.banks[(it % 2) * 2 + 1], "ps%d" % ((it % 2) * 2 + 1)
            rcv, rcn = rc[it % 2], "rc%d" % (it % 2)
            otv, otn = ot[it % 2], "ot%d" % (it % 2)

            def s_step(kc):
                sb_i = 4 + (kc % 4)
                sbk, sbn = self.banks[sb_i], "ps%d" % sb_i
                ptv, ptn = PT[kc % 4], "PT%d" % (kc % 4)
                p.op("pe", lambda e: e.matmul(
                    sbk[:, :], KT[:, g, kc * 128:(kc + 1) * 128], qv[:, qh, qb * 512:(qb + 1) * 512],
                    start=True, stop=True), reads=["KT", qn], writes=[sbn])
                p.op("act", lambda e: e.activation(ptv, sbk[:, :], AF.Exp, scale=scale),
                     reads=[sbn], writes=[ptn])

            def pv_step(kc):
                ptv, ptn = PT[kc % 4], "PT%d" % (kc % 4)
                p.op("pe", lambda e: e.matmul(
                    bo[:, :], Vt[:, kc, g * 128:(g + 1) * 128], ptv, start=(kc == 0), stop=(kc == 15)),
                    reads=["Vt", ptn], writes=[bon])
                p.op("pe", lambda e: e.matmul(
                    bs[:, :], ones[:], ptv, start=(kc == 0), stop=(kc == 15)),
                    reads=["onesb", ptn], writes=[bsn])

            s_step(0)
            s_step(1)
            for kc in range(16):
                if kc + 2 < 16:
                    s_step(kc + 2)
                pv_step(kc)
            p.op("dve", lambda e: e.reciprocal(rcv, bs[:, :]), reads=[bsn], writes=[rcn])
            p.op("dve", lambda e: e.tensor_tensor(otv, bo[:, :], rcv, ALU.mult), reads=[bon, rcn], writes=[otn])
            p.dma("sp", self.mo[:, h, qb * 512:(qb + 1) * 512], otv, self.sem(otn), reads=[otn],
                  writes=["mo:%d:%d" % (h, qb)])

        it = 0
        for g in range(4):
            qv, qn = qg[g % 2], "qg%d" % (g % 2)
            p.dma("sp", qv, self.qs[:, 4 * g:4 * g + 4, :], self.sem(qn), reads=[], writes=[qn])
            for qh in range(4):
                for qb in range(4):
                    attend(g, qh, qb, it, qv, qn)
                    it += 1
        p.barrier()

    def phase_na(self, j):
        p = self.p
        ones, identb = self.onesb, self.identb
        xn = self.xn_view()
        xn_names = ["xn:%d" % b for b in range(NBLK)]
        Vt = [self.V(65536 + i * 8192, BF16, [128, 16, 256]) for i in range(2)]
        o = SCR_OFF
        qT = [self.V(o + i * 4096, BF16, [128, S]) for i in range(2)]; o += 8192
        kT = [self.V(o + i * 4096, BF16, [128, S]) for i in range(2)]; o += 8192
        bias = [self.V(o + i * 5376, BF16, [128, NA_NTAB * 128]) for i in range(2)]; o += 10752
        PT = [self.V(o + i * 1280, BF16, [128, 640]) for i in range(2)]; o += 2560
        rc = [self.V(o + i * 512, F32, [128, 128]) for i in range(2)]; o += 1024
        at = [self.V(o + i * 4096, BF16, [128, S]) for i in range(2)]; o += 8192
        scale = 128.0 ** -0.5
        banks = self.banks
        pj = [0]

        def proj_bank():
            i = pj[0] % 2
            pj[0] += 1
            return banks[i], "ps%d" % i

        def head(hp, ci, slq, slk, vt, vtn):
            h = 2 * hp + ci
            qv, qn = qT[h % 2], "qT%d" % (h % 2)
            kv, kn = kT[h % 2], "kT%d" % (h % 2)
            bv, bn = bias[h % 2], "bias%d" % (h % 2)
            av, an = at[h % 2], "at%d" % (h % 2)
            p.dma("pool", bv, self.na_bias[j, h], self.sem(bn), reads=[], writes=[bn])
            for (slab, dst, dn, isq) in ((slq, qv, qn, True), (slk, kv, kn, False)):
                sv, sname = slab
                sv3 = sv.rearrange("p (k n) -> p k n", n=256)
                for tb in range(4):
                    bk, bname = proj_bank()
                    for kc in range(16):
                        p.op("pe", lambda e, bk=bk, sv3=sv3, kc=kc, tb=tb: e.matmul(
                            bk[:, :], sv3[:, kc, ci * 128:(ci + 1) * 128], xn[:, kc, tb * 512:(tb + 1) * 512],
                            start=(kc == 0), stop=(kc == 15)),
                            reads=[sname] + xn_names[2 * tb:2 * tb + 2], writes=[bname])
                    if isq:
                        p.op("act", lambda e, bk=bk, dst=dst, tb=tb: e.activation(
                            dst[:, tb * 512:(tb + 1) * 512], bk[:, :], AF.Copy, scale=scale),
                            reads=[bname], writes=[dn])
                    else:
                        p.op("dve", lambda e, bk=bk, dst=dst, tb=tb: e.tensor_copy(
                            dst[:, tb * 512:(tb + 1) * 512], bk[:, :]), reads=[bname], writes=[dn])

            def qk(m):
                tbase, nch, kc0 = _na_pair(m)
                A, An = banks[2 + m % 2], "ps%d" % (2 + m % 2)
                B, Bn = banks[4 + m % 2], "ps%d" % (4 + m % 2)
                ptv, ptn = PT[m % 2], "PT%d" % (m % 2)
                for c in range(nch):
                    dst = A[:, c * 128:(c + 1) * 128] if c < 4 else B[:, 0:128]
                    dn = An if c < 4 else Bn
                    p.op("pe", lambda e, dst=dst, c=c: e.matmul(
                        dst, kv[:, (kc0 + c) * 128:(kc0 + c + 1) * 128], qv[:, m * 128:(m + 1) * 128],
                        start=True, stop=False), reads=[kn, qn], writes=[dn])
                    p.op("pe", lambda e, dst=dst, c=c: e.matmul(
                        dst, identb[:], bv[:, (tbase + c) * 128:(tbase + c + 1) * 128],
                        start=False, stop=True), reads=["identb", bn], writes=[dn])
                p.op("act", lambda e: e.activation(ptv[:, 0:512], A[:, :], AF.Exp), reads=[An], writes=[ptn])
                if nch == 5:
                    p.op("act", lambda e: e.activation(ptv[:, 512:640], B[:, 0:128], AF.Exp), reads=[Bn], writes=[ptn])

            def pv(m):
                tbase, nch, kc0 = _na_pair(m)
                O, On = banks[6 + m % 2], "ps%d" % (6 + m % 2)
                ptv, ptn = PT[m % 2], "PT%d" % (m % 2)
                rcv, rcn = rc[m % 2], "rc%d" % (m % 2)
                for c in range(nch):
                    p.op("pe", lambda e, c=c: e.matmul(
                        O[:, 0:128], vt[:, kc0 + c, ci * 128:(ci + 1) * 128], ptv[:, c * 128:(c + 1) * 128],
                        start=(c == 0), stop=(c == nch - 1)), reads=[vtn, ptn], writes=[On])
                for c in range(nch):
                    p.op("pe", lambda e, c=c: e.matmul(
                        O[:, 128:256], ones[:], ptv[:, c * 128:(c + 1) * 128],
                        start=(c == 0), stop=(c == nch - 1)), reads=["onesb", ptn], writes=[On])
                p.op("dve", lambda e: e.reciprocal(rcv, O[:, 128:256]), reads=[On], writes=[rcn])
                p.op("dve", lambda e: e.tensor_tensor(av[:, m * 128:(m + 1) * 128], O[:, 0:128], rcv, ALU.mult),
                     reads=[On, rcn], writes=[an])

            qk(0)
            for m in range(16):
                if m + 1 < 16:
                    qk(m + 1)
                pv(m)
            p.dma("sp", self.mo[:, h, :], av, self.sem(an), reads=[an], writes=["mo:%d" % h])

        for hp in range(8):
            slq = self.load_slab(self.na_qkv[j, hp], 4096)
            slk = self.load_slab(self.na_qkv[j, 8 + hp], 4096)
            slv = self.load_slab(self.na_qkv[j, 16 + hp], 4096)
            vt, vtn = Vt[hp % 2], "Vt%d" % (hp % 2)
            sv3 = slv[0].rearrange("p (k n) -> p k n", n=256)
            for tc in range(16):
                bk, bname = proj_bank()
                for kc in range(16):
                    p.op("pe", lambda e, bk=bk, sv3=sv3, kc=kc, tc=tc: e.matmul(
                        bk[:, 0:256], xn[:, kc, tc * 128:(tc + 1) * 128], sv3[:, kc, :],
                        start=(kc == 0), stop=(kc == 15)),
                        reads=[slv[1], xn_names[tc // 2]], writes=[bname])
                p.op("act", lambda e, bk=bk, tc=tc, vt=vt: e.activation(vt[:, tc, :], bk[:, 0:256], AF.Copy),
                     reads=[bname], writes=[vtn])
            for ci in range(2):
                head(hp, ci, slq, slk, vt, vtn)
        p.barrier()

    def build(self):
        cfg = self.cfg
        layers = cfg.get("layers", [0, 1, 2, 3])
        self.consts()
        first = True
        for l in layers:
            m, j = l % 3, l // 3
            if cfg.get("mixer", True):
                if first:
                    self.phase_input(CV_MIX + 16 * l)
                else:
                    self.phase_norm(CV_MIX + 16 * l)
                first = False
                if m == 0:
                    self.phase_na(j)
                    self.phase_linear([[(self.na_o[j, s_], 4096, 0, 16)] for s_ in range(8)], 16, 256, self.src_mo, 1)
                elif m == 1:
                    self.phase_sc()
                    self.phase_linear([[(self.sc_o[s_], 4096, 0, 16)] for s_ in range(8)], 16, 256, self.src_mo, 1)
                else:
                    self.phase_gqa()
                    self.phase_linear([[(self.gq_o[s_], 4096, 0, 16)] for s_ in range(8)], 16, 256, self.src_mo, 1)
            if cfg.get("ffn", True):
                if first:
                    self.phase_input(CV_FFN + 16 * l)
                else:
                    self.phase_norm(CV_FFN + 16 * l)
                first = False
                self.phase_ffn_up(l)
                self.phase_linear([[(self.w_dn[l, s_, kh], 2816, 22 * kh, 22) for kh in range(2)] for s_ in range(16)],
                                  NCHF, 128, self.src_hm, 2)
        if first:
            self.phase_input(CV_FIN)
        if cfg.get("dump_h"):
            self.phase_dump()
        self.phase_output(CV_FIN)
        self.p.finish()
        self.es.close()
        return self.nc

    def src_mo(self, hf, k0, k1):
        return self.mo[:, k0:k1, :]

    def src_hm(self, hf, k0, k1):
        return self.hm[hf, :, k0:k1, :]

    def phase_dump(self):
        p = self.p
        for c in range(16):
            p.dma("sp", self.hdump[c], self.hT[c], self.sem("dump%d" % c), reads=[], writes=["dump%d" % c])
        p.barrier()


def _tile_w(W, sw):
    K, N = W.shape
    kc = K // 128
    t = W.reshape(kc, 128, N // sw, sw).transpose(2, 1, 0, 3)
    return np.ascontiguousarray(t).reshape(N // sw, 128, kc * sw)


def _tile_wdn(W):
    t = W.reshape(2, 22, 128, 16, 128).transpose(3, 0, 2, 1, 4)
    return np.ascontiguousarray(t).reshape(16, 2, 128, 22 * 128)


def _fm(v):
    v = np.asarray(v, np.float32)
    lead = v.shape[:-1]
    c = v.shape[-1] // 128
    t = v.reshape(*lead, c, 128)
    t = np.moveaxis(t, -1, 0)
    return np.ascontiguousarray(t).reshape(128, -1)


def _na_bias_tables(rpb):
    H = rpb.shape[0]
    out = np.empty((H, 128, NA_NTAB, 128), np.float32)
    pk = np.arange(128)[:, None]
    pq = np.arange(128)[None, :]
    specs = [((0, 1), 0, 4, 0), ((2, 3), 0, 4, 4), ((4, 5), 0, 5, 8), ((28, 29), 24, 4, 13), ((30, 31), 24, 4, 17)]
    for (r0, r1), a, nch, base in specs:
        r = np.where(pq < 64, r0, r1)
        qcol = pq % 64
        r_start = np.clip(r - 4, 0, 24)
        c_start = np.clip(qcol - 8, 0, 48)
        for c in range(nch):
            krow = a + 2 * c + pk // 64
            kcol = pk % 64
            valid = (krow >= r_start) & (krow < r_start + 8) & (kcol >= c_start) & (kcol < c_start + 16)
            dr = np.clip(krow - r + 7, 0, 14)
            dc = np.clip(kcol - qcol + 15, 0, 30)
            g = rpb[:, dr, dc]
            out[:, :, base + c, :] = np.where(valid[None], g, np.float32(-30000.0))
    return out.reshape(H, 128, NA_NTAB * 128)


def _rope_tables():
    t = np.arange(S)
    row = (t // 64).astype(np.float32)[:, None]
    col = (t % 64).astype(np.float32)[:, None]
    inv = (np.float32(10000.0) ** (-np.arange(0, 64, 2, dtype=np.float32) / np.float32(64))).astype(np.float32)
    ang = np.concatenate([row * inv, row * inv, col * inv, col * inv], axis=-1)
    return np.ascontiguousarray(np.stack([np.cos(ang).T, np.sin(ang).T]).astype(np.float32))


def _rot_matrix():
    R = np.zeros((128, 128), np.float32)
    for do in range(128):
        if (do % 64) < 32:
            R[do + 32, do] = -1.0
        else:
            R[do - 32, do] = 1.0
    return R


def prep_shared(inp):
    f = lambda a: np.asarray(a, np.float32)
    sh = {}
    cvp = [_fm(f(inp["mix_norm"])), _fm(f(inp["ffn_norm"])), _fm(f(inp["final_norm"])),
           _fm(f(inp["ffn_conv_w"])), _fm(f(inp["ffn_conv_b"])), _fm(f(inp["sc_conv_w"])[0]),
           f(inp["gqa_q_norm"])[0].reshape(128, 1), f(inp["gqa_k_norm"])[0].reshape(128, 1)]
    sh["cv"] = np.ascontiguousarray(np.concatenate(cvp, axis=1))
    assert sh["cv"].shape == (128, NCV), sh["cv"].shape
    sh["ident"] = np.eye(128, dtype=np.float32)
    sh["rot"] = _rot_matrix()
    sh["cs"] = _rope_tables()
    sh["w_up"] = np.stack([_tile_w(f(inp["ffn_w_up"])[l], 256) for l in range(4)])
    sh["w_dn"] = np.stack([_tile_wdn(f(inp["ffn_w_down"])[l]) for l in range(4)])
    sh["na_qkv"] = np.stack([_tile_w(f(inp["na_w_qkv"])[j], 256) for j in range(2)])
    sh["na_o"] = np.stack([_tile_w(f(inp["na_w_o"])[j], 256) for j in range(2)])
    sh["na_bias"] = np.stack([_na_bias_tables(f(inp["na_rpb"])[j]) for j in range(2)])
    sh["sc_in"] = _tile_w(f(inp["sc_w_in"])[0], 256)
    sh["sc_o"] = _tile_w(f(inp["sc_w_out"])[0], 256)
    sh["gq_qkv"] = _tile_w(f(inp["gqa_w_qkv"])[0], 256)
    sh["gq_o"] = _tile_w(f(inp["gqa_w_o"])[0], 256)
    return sh


_CFG = {"layers": [0, 1, 2, 3]}


def kernel(**inputs):
    x = np.asarray(inputs["x"], np.float32)
    sh = prep_shared(inputs)
    nc = Builder(dict(_CFG)).build()
    n = x.shape[0]
    in_maps = [dict(sh, x=np.ascontiguousarray(x[b])) for b in range(n)]
    res = run_bass_kernel_spmd(nc, in_maps, core_ids=list(range(n)))
    return np.stack([np.asarray(r["out"], np.float32) for r in res.results], axis=0)
```

```python
import numpy as np
import concourse.bass as bass
import concourse.mybir as mybir
from concourse.bass_utils import run_bass_kernel_spmd
from contextlib import ExitStack

F32 = mybir.dt.float32
BF16 = mybir.dt.bfloat16
AF = mybir.ActivationFunctionType
ALU = mybir.AluOpType


class _Op:
    __slots__ = ("eng", "fn", "deps", "needed", "sig", "is_dma")

    def __init__(self, eng, fn, deps, is_dma):
        self.eng = eng
        self.fn = fn
        self.deps = deps
        self.needed = False
        self.sig = None
        self.is_dma = is_dma


class _DSem:
    def __init__(self, sem):
        self.sem = sem
        self.count = 0


class Prog:
    ENG = ("pe", "act", "dve", "pool", "sp")
    SEM_LIMIT = 30000

    def __init__(self, nc, es):
        self.nc = nc
        self.es = es
        self.ops = {e: [] for e in self.ENG}
        self.res = {}
        self.esem = {e: es.enter_context(nc.semaphore("sem_" + e)) for e in self.ENG}
        self.nwaits = 0
        self.last_real = {}
        self.pending_dma = []

    def sbuf(self, name, shape, dt):
        return self.es.enter_context(self.nc.sbuf_tensor(name, shape, dt))

    def psum(self, name, shape, dt):
        return self.es.enter_context(self.nc.psum_tensor(name, shape, dt))

    def dsem(self, name):
        return _DSem(self.es.enter_context(self.nc.semaphore(name)))

    def _record(self, eng, fn, reads, writes, is_dma):
        deps = []
        for r in reads:
            st = self.res.get(r)
            if st is not None and st[0] is not None:
                deps.append(st[0])
        for w in writes:
            st = self.res.get(w)
            if st is not None:
                if st[0] is not None:
                    deps.append(st[0])
                deps.extend(st[1].values())
                deps.extend(st[2])
        op = _Op(eng, fn, deps, is_dma)
        for d in deps:
            if not (d.eng == "pe" and eng == "pe" and not d.is_dma and not is_dma):
                d.needed = True
        for r in reads:
            st = self.res.get(r)
            if st is None:
                st = self.res[r] = [None, {}, []]
            if is_dma:
                st[2].append(op)
            else:
                st[1][eng] = op
        for w in writes:
            self.res[w] = [op, {}, []]
        self.ops[eng].append(op)
        if fn is not None:
            if is_dma:
                self.pending_dma.append(op)
            else:
                self.last_real[eng] = op
        return op

    def op(self, eng, fn, reads=(), writes=()):
        return self._record(eng, fn, reads, writes, False)

    def dma(self, eng, out, in_, dsem, reads=(), writes=()):
        op = self._record(eng, lambda e: e.dma_start(out=out, in_=in_), reads, writes, True)
        if dsem.count + 16 > self.SEM_LIMIT:
            dsem.sem = self.es.enter_context(self.nc.semaphore())
            dsem.count = 0
        dsem.count += 16
        op.sig = (dsem.sem, dsem.count)
        return op

    def finish(self, final_waits=()):
        self.op("sp", None, reads=list(final_waits))
        for e in self.ENG:
            c = 0
            sem = self.esem[e]
            for op in self.ops[e]:
                if not op.is_dma and op.needed:
                    if c >= self.SEM_LIMIT:
                        sem = self.es.enter_context(self.nc.semaphore())
                        c = 0
                    c += 1
                    op.sig = (sem, c)
        engs = {"pe": "tensor", "act": "scalar", "dve": "vector", "pool": "gpsimd", "sp": "sync"}
        with self.nc.Block() as block:
            for e in self.ENG:
                getattr(block, engs[e])(lambda eng, e=e: self._emit(e, eng))

    def _emit(self, e, eng):
        waited = {}
        for op in self.ops[e]:
            for d in op.deps:
                if d.eng == "pe" and e == "pe" and not d.is_dma and not op.is_dma:
                    continue
                sem, val = d.sig
                key = id(sem)
                if waited.get(key, 0) >= val:
                    continue
                eng.wait_ge(sem, val)
                self.nwaits += 1
                waited[key] = val
            if op.fn is None:
                continue
            ins = op.fn(eng)
            if op.is_dma:
                ins.then_inc(op.sig[0], 16)
            elif op.needed:
                ins.then_inc(op.sig[0], 1)

    def barrier(self):
        lasts = [self.last_real[e] for e in self.ENG if self.last_real.get(e) is not None]
        lasts += self.pending_dma
        for e in self.ENG:
            op = _Op(e, None, list(lasts), False)
            self.ops[e].append(op)
        for d in lasts:
            d.needed = True
        self.pending_dma = []
        self.res = {}


S = 2048
D = 2048
DFF = 5632
NH = 16
EPS = 1e-6
NBLK = 8
BT = 256
NCHF = DFF // 128

CV_MIX = 0
CV_FFN = 64
CV_FIN = 128
CV_FCW = 144
CV_FCB = CV_FCW + 4 * 3 * 88
CV_SCW = CV_FCB + 4 * 88
CV_QN = CV_SCW + 48
CV_KN = CV_QN + 1
NCV = CV_KN + 1

BIG_B = 90112
RSLOT_B = 8192
NRING = 6
SCR_OFF = BIG_B + NRING * RSLOT_B
M_BYTES = 204800
RSTD_OFF = M_BYTES - 8192

NA_VAR = {0: (0, 4), 1: (4, 4), "int": (8, 5), 14: (13, 4), 15: (17, 4)}
NA_NTAB = 21


def _na_pair(m):
    if m in (0, 1):
        tb, nch = NA_VAR[m]
        return tb, nch, 0
    if m in (14, 15):
        tb, nch = NA_VAR[m]
        return tb, nch, 12
    tb, nch = NA_VAR["int"]
    return tb, nch, m - 2


class Builder:
    def __init__(self, cfg):
        self.cfg = cfg
        nc = self.nc = bass.Bass("TRN2", target_bir_lowering=False)
        self.es = ExitStack()
        p = self.p = Prog(nc, self.es)

        def din(name, shape, dt=F32):
            return nc.dram_tensor(name, shape, dt, kind="ExternalInput").ap()

        def dscr(name, shape, dt):
            return nc.dram_tensor(name, shape, dt, kind="Internal").ap()

        self.x = din("x", [S, D])
        self.cv = din("cv", [128, NCV])
        self.ident = din("ident", [128, 128])
        self.rot = din("rot", [128, 128])
        self.cs = din("cs", [2, 128, S])
        self.w_up = din("w_up", [4, 44, 128, 4096])
        self.w_dn = din("w_dn", [4, 16, 2, 128, 2816])
        self.na_qkv = din("na_qkv", [2, 24, 128, 4096])
        self.na_o = din("na_o", [2, 8, 128, 4096])
        self.na_bias = din("na_bias", [2, NH, 128, NA_NTAB * 128])
        self.sc_in = din("sc_in", [24, 128, 4096])
        self.sc_o = din("sc_o", [8, 128, 4096])
        self.gq_qkv = din("gq_qkv", [12, 128, 4096])
        self.gq_o = din("gq_o", [8, 128, 4096])
        self.out = nc.dram_tensor("out", [S, D], F32, kind="ExternalOutput").ap()
        self.hT = dscr("hT", [16, 128, S], F32)
        self.hm = dscr("hm", [2, 128, NCHF, S // 2], BF16)
        self.mo = dscr("mo", [128, 16, S], BF16)
        self.qs = dscr("qs", [128, 16, S], BF16)
        if cfg.get("dump_h"):
            self.hdump = nc.dram_tensor("hdump", [16, 128, S], F32, kind="ExternalOutput").ap()

        self.M = p.sbuf("M", [128, M_BYTES // 4], F32)
        self.cvt = p.sbuf("cvt", [128, NCV], F32)
        self.identf = p.sbuf("identf", [128, 128], F32)
        self.identb = p.sbuf("identb", [128, 128], BF16)
        self.onesb = p.sbuf("onesb", [128, 128], BF16)
        self.rotb = p.sbuf("rotb", [128, 128], BF16)
        self.epsb = p.sbuf("epsb", [128, 1], F32)
        self.banks = [p.psum("ps%d" % i, [128, 512], F32) for i in range(8)]
        self.bank_i = 0
        self.bank_pool = list(range(8))
        self.sems = {}
        self.ring_i = 0

    def V(self, off, dt, shape):
        n = 1
        for s_ in shape[1:]:
            n *= s_
        esz = 4 if dt == F32 else 2
        assert off % 4 == 0 and (n * esz) % 4 == 0
        assert off + n * esz <= M_BYTES, (off, n, esz)
        ap = self.M[:, off // 4:(off + n * esz) // 4]
        if dt != F32:
            ap = ap.bitcast(dt)
        if len(shape) == 3:
            ap = ap.rearrange("p (a b) -> p a b", b=shape[2])
        return ap

    def sem(self, name):
        if name not in self.sems:
            self.sems[name] = self.p.dsem("d_" + name.replace(":", "_"))
        return self.sems[name]

    def bank(self):
        i = self.bank_pool[self.bank_i % len(self.bank_pool)]
        self.bank_i += 1
        return self.banks[i], "ps%d" % i

    def ring_slot(self):
        i = self.ring_i
        self.ring_i = (i + 1) % NRING
        return i

    def load_slab(self, src, nel):
        i = self.ring_slot()
        name = "ring%d" % i
        v = self.V(BIG_B + i * RSLOT_B, BF16, [128, nel])
        self.p.dma("pool", v, src, self.sem(name), reads=[], writes=[name])
        return v, name

    def consts(self):
        p = self.p
        p.dma("sp", self.cvt[:], self.cv, self.sem("cvt"), writes=["cvt"])
        p.dma("sp", self.identf[:], self.ident, self.sem("identf"), writes=["identf"])
        p.dma("pool", self.identb[:], self.ident, self.sem("identb"), writes=["identb"])
        p.dma("pool", self.rotb[:], self.rot, self.sem("rotb"), writes=["rotb"])
        p.op("pool", lambda e: e.memset(self.onesb[:], 1.0), writes=["onesb"])
        p.op("pool", lambda e: e.memset(self.epsb[:], EPS), writes=["epsb"])
        p.barrier()

    def rstd_view(self):
        return self.V(RSTD_OFF, F32, [128, S])

    def norm_apply(self, hblk, hname, b, gcol, out_ap, out_name):
        p = self.p
        cvt = self.cvt
        rsv = self.rstd_view()[:, b * BT:(b + 1) * BT]
        for c in range(16):
            p.op("dve", lambda e, c=c: e.scalar_tensor_tensor(
                out_ap[:, c, :], hblk[:, c, :], cvt[:, gcol + c:gcol + c + 1], rsv, ALU.mult, ALU.mult),
                reads=[hname, "rstd:%d" % (b // 2), "cvt"], writes=[out_name])

    def norm_stats(self, hblk, hname, b, sqv, sdv):
        p = self.p
        ones, epsb = self.onesb, self.epsb
        rsv = self.rstd_view()[:, b * BT:(b + 1) * BT]
        p.op("act", lambda e: e.activation(sqv, hblk, AF.Square), reads=[hname], writes=["sq"])
        bk, bname = self.bank()
        for c in range(16):
            p.op("pe", lambda e, c=c: e.matmul(bk[:, 0:BT], ones[:], sqv[:, c, :], start=(c == 0), stop=(c == 15)),
                 reads=["sq", "onesb"], writes=[bname])
        p.op("act", lambda e: e.activation(sdv, bk[:, 0:BT], AF.Sqrt, bias=epsb[:, 0:1], scale=1.0 / D),
             reads=[bname, "epsb"], writes=["sd"])
        p.op("dve", lambda e: e.reciprocal(rsv, sdv), reads=["sd"], writes=["rstd:%d" % (b // 2)])

    def xn_view(self):
        return self.V(0, BF16, [128, 16, S])

    def xg_chunk(self, c):
        if c < 6:
            return self.V(65536 + c * 4096, BF16, [128, S])
        return self.V(SCR_OFF + 14336 + (c - 6) * 4096, BF16, [128, S])

    def phase_input(self, gcol):
        p = self.p
        o = SCR_OFF
        hb = [self.V(o + i * 16384, F32, [128, 16, BT]) for i in range(2)]
        o += 32768
        xt = self.V(o, F32, [128, 2048]); o += 8192
        sq = self.V(o, BF16, [128, 16, BT]); o += 8192
        sd = self.V(o, F32, [128, BT]); o += 1024
        xn = self.xn_view()
        identf = self.identf
        for b in range(NBLK):
            hblk, hname = hb[b % 2], "hb%d" % (b % 2)
            for tl in range(2):
                r0 = (b * 2 + tl) * 128
                p.dma("sp", xt, self.x[r0:r0 + 128, :], self.sem("xt"), writes=["xt"])
                for cg in range(4):
                    bk, bname = self.bank()
                    for ci in range(4):
                        c = cg * 4 + ci
                        p.op("pe", lambda e, c=c, ci=ci, bk=bk: e.transpose(
                            bk[:, ci * 128:(ci + 1) * 128], xt[:, c * 128:(c + 1) * 128], identf[:]),
                            reads=["xt", "identf"], writes=[bname])
                    dst = hblk[:, cg * 4:(cg + 1) * 4, tl * 128:(tl + 1) * 128]
                    src = bk[:, :].rearrange("p (a b) -> p a b", b=128)
                    if cg % 2 == 0:
                        p.op("act", lambda e, dst=dst, src=src: e.activation(dst, src, AF.Copy),
                             reads=[bname], writes=[hname])
                    else:
                        p.op("dve", lambda e, dst=dst, src=src: e.tensor_copy(dst, src),
                             reads=[bname], writes=[hname])
            p.dma("sp", self.hT[:, :, b * BT:(b + 1) * BT].rearrange("c p t -> p c t"), hblk,
                  self.sem(hname + "st"), reads=[hname], writes=["hT:b%d" % b])
            self.norm_stats(hblk, hname, b, sq, sd)
            self.norm_apply(hblk, hname, b, gcol, xn[:, :, b * BT:(b + 1) * BT], "xn:%d" % b)
        p.barrier()

    def phase_norm(self, gcol, barrier=True):
        p = self.p
        cvt = self.cvt
        hb = [self.V(65536 + i * 8192, F32, [128, 4, 512]) for i in range(3)]
        xn = self.xn_view()
        rstd = self.rstd_view()
        k = 0
        for tb in range(4):
            for cg in range(4):
                buf, name = hb[k % 3], "nb%d" % (k % 3)
                k += 1
                p.dma("sp", buf, self.hT[cg * 4:(cg + 1) * 4, :, tb * 512:(tb + 1) * 512].rearrange("c p t -> p c t"),
                      self.sem(name), reads=[], writes=[name])
                for ci in range(4):
                    c = cg * 4 + ci
                    p.op("dve", lambda e, buf=buf, ci=ci, c=c, tb=tb: e.scalar_tensor_tensor(
                        xn[:, c, tb * 512:(tb + 1) * 512], buf[:, ci, :], cvt[:, gcol + c:gcol + c + 1],
                        rstd[:, tb * 512:(tb + 1) * 512], ALU.mult, ALU.mult),
                        reads=[name, "rstd:%d" % tb, "cvt"], writes=["xn:%d" % (2 * tb), "xn:%d" % (2 * tb + 1)])
        if barrier:
            p.barrier()

    def phase_output(self, gcol):
        p = self.p
        NB = 3
        hb = [self.V(SCR_OFF + i * 16384, F32, [128, 16, BT]) for i in range(NB)]
        identf = self.identf
        ot = [self.V(0 + i * 8192, F32, [128, 2048]) for i in range(2)]
        k = 0
        for b in range(NBLK):
            hblk, hname = hb[b % NB], "hb%d" % (b % NB)
            p.dma("sp", hblk, self.hT[:, :, b * BT:(b + 1) * BT].rearrange("c p t -> p c t"),
                  self.sem(hname), reads=[], writes=[hname])
            self.norm_apply(hblk, hname, b, gcol, hblk, hname)
            for tl in range(2):
                otile, oname = ot[k % 2], "ot%d" % (k % 2)
                k += 1
                for cg in range(4):
                    bk, bname = self.bank()
                    for ci in range(4):
                        c = cg * 4 + ci
                        p.op("pe", lambda e, c=c, ci=ci, bk=bk, tl=tl, hblk=hblk: e.transpose(
                            bk[:, ci * 128:(ci + 1) * 128], hblk[:, c, tl * 128:(tl + 1) * 128], identf[:]),
                            reads=[hname, "identf"], writes=[bname])
                    dst = otile[:, cg * 512:(cg + 1) * 512]
                    if cg % 2 == 0:
                        p.op("act", lambda e, dst=dst, bk=bk: e.activation(dst, bk[:, :], AF.Copy),
                             reads=[bname], writes=[oname])
                    else:
                        p.op("dve", lambda e, dst=dst, bk=bk: e.tensor_copy(dst, bk[:, :]),
                             reads=[bname], writes=[oname])
                r0 = (b * 2 + tl) * 128
                p.dma("sp", self.out[r0:r0 + 128, :], otile, self.sem(oname), reads=[oname], writes=["out:%d" % r0])
        p.barrier()

    def phase_ffn_up(self, l, fused=False):
        p = self.p
        cvt = self.cvt
        if fused:
            xnc = [self.xg_chunk(c) for c in range(16)]
            o = 0
        else:
            xn = self.xn_view()
            xnc = [xn[:, c, :] for c in range(16)]
            o = SCR_OFF
        rstd = self.rstd_view()
        yraw = [self.V(o + i * 8200, F32, [128, 2050]) for i in range(2)]
        o += 2 * 8200
        cc = [self.V(o + i * 8192, F32, [128, 2048]) for i in range(2)]
        o += 2 * 8192
        hmt = [self.V(o + i * 4096, BF16, [128, 2048]) for i in range(2)]
        for i in range(2):
            p.op("pool", lambda e, i=i: e.memset(yraw[i][:, 0:1], 0.0), writes=["yr%d" % i])
            p.op("pool", lambda e, i=i: e.memset(yraw[i][:, 2049:2050], 0.0), writes=["yr%d" % i])
        xn_names = ["xn:%d" % b for b in range(NBLK)]
        for s in range(22):
            slabs = [self.load_slab(self.w_up[l, s], 4096), self.load_slab(self.w_up[l, 22 + s], 4096)]
            for ci in range(2):
                j = 2 * s + ci
                for st in range(2):
                    sv, sname = slabs[st]
                    sv3 = sv.rearrange("p (k n) -> p k n", n=256)
                    ch = j + st * NCHF
                    w0 = cvt[:, CV_FCW + (l * 3 + 0) * 88 + ch:CV_FCW + (l * 3 + 0) * 88 + ch + 1]
                    w1 = cvt[:, CV_FCW + (l * 3 + 1) * 88 + ch:CV_FCW + (l * 3 + 1) * 88 + ch + 1]
                    w2 = cvt[:, CV_FCW + (l * 3 + 2) * 88 + ch:CV_FCW + (l * 3 + 2) * 88 + ch + 1]
                    bb = cvt[:, CV_FCB + l * 88 + ch:CV_FCB + l * 88 + ch + 1]
                    yr, yname = yraw[st], "yr%d" % st
                    cv_, cname = cc[st], "cc%d" % st
                    for tb in range(4):
                        bk, bname = self.bank()
                        for kc in range(16):
                            p.op("pe", lambda e, bk=bk, sv3=sv3, kc=kc, ci=ci, tb=tb: e.matmul(
                                bk[:, :], sv3[:, kc, ci * 128:(ci + 1) * 128], xnc[kc][:, tb * 512:(tb + 1) * 512],
                                start=(kc == 0), stop=(kc == 15)),
                                reads=[sname] + xn_names[2 * tb:2 * tb + 2], writes=[bname])
                        if fused:
                            p.op("dve", lambda e, bk=bk, yr=yr, tb=tb: e.tensor_tensor(
                                yr[:, 1 + tb * 512:1 + (tb + 1) * 512], bk[:, :], rstd[:, tb * 512:(tb + 1) * 512], ALU.mult),
                                reads=[bname, "rstd:%d" % tb], writes=[yname])
                            p.op("act", lambda e, yr=yr, cv_=cv_, tb=tb, w1=w1, bb=bb: e.activation(
                                cv_[:, tb * 512:(tb + 1) * 512], yr[:, 1 + tb * 512:1 + (tb + 1) * 512], AF.Identity,
                                bias=bb, scale=w1),
                                reads=[yname, "cvt"], writes=[cname])
                        else:
                            p.op("act", lambda e, bk=bk, yr=yr, tb=tb: e.activation(
                                yr[:, 1 + tb * 512:1 + (tb + 1) * 512], bk[:, :], AF.Copy),
                                reads=[bname], writes=[yname])
                            p.op("act", lambda e, bk=bk, cv_=cv_, tb=tb, w1=w1, bb=bb: e.activation(
                                cv_[:, tb * 512:(tb + 1) * 512], bk[:, :], AF.Identity, bias=bb, scale=w1),
                                reads=[bname, "cvt"], writes=[cname])
                    p.op("dve", lambda e, yr=yr, cv_=cv_, w0=w0: e.scalar_tensor_tensor(
                        cv_[:, :], yr[:, 0:2048], w0, cv_[:, :], ALU.mult, ALU.add),
                        reads=[yname, cname, "cvt"], writes=[cname])
                    p.op("dve", lambda e, yr=yr, cv_=cv_, w2=w2: e.scalar_tensor_tensor(
                        cv_[:, :], yr[:, 2:2050], w2, cv_[:, :], ALU.mult, ALU.add),
                        reads=[yname, cname, "cvt"], writes=[cname])
                p.op("act", lambda e: e.activation(cc[0][:, :], cc[0][:, :], AF.Silu), reads=["cc0"], writes=["cc0"])
                ht, hname = hmt[j % 2], "hmt%d" % (j % 2)
                p.op("dve", lambda e, ht=ht: e.tensor_tensor(ht[:, :], cc[0][:, :], cc[1][:, :], ALU.mult),
                     reads=["cc0", "cc1"], writes=[hname])
                for hf in range(2):
                    p.dma("sp", self.hm[hf, :, j, :], ht[:, hf * 1024:(hf + 1) * 1024], self.sem(hname + "h%d" % hf),
                          reads=[hname], writes=["hm:%d:%d" % (j, hf)])
        p.barrier()

    def phase_linear(self, groups, KC, SW, src_fn, nhalf, xg_gcol=None):
        p = self.p
        ones, epsb = self.onesb, self.epsb
        TH = S // nhalf
        ntb = TH // 512
        inT = self.V(0, BF16, [128, KC, TH])
        NHB = 4
        o = SCR_OFF
        hold = [self.V(o + i * 2048, F32, [128, 512]) for i in range(NHB)]
        o += NHB * 2048
        sqt = [self.V(o + i * 1024, BF16, [128, 512]) for i in range(NHB)]
        o += NHB * 1024
        sdv = self.V(o, F32, [128, 512]); o += 2048
        rstd = self.rstd_view()
        nci = SW // 128
        ngr = len(groups)
        for hf in range(nhalf):
            t0 = hf * TH
            ssq_banks = [7 - tb for tb in range(ntb)]
            self.bank_pool = [i for i in range(8) if i not in ssq_banks]
            bounds = [0, 4, 16] + ([KC] if KC > 16 else [])
            in_names = [None] * KC
            for gi in range(len(bounds) - 1):
                k0, k1 = bounds[gi], bounds[gi + 1]
                nm = "inT:%d" % k0
                for kc_ in range(k0, k1):
                    in_names[kc_] = nm
                p.dma("sp", inT[:, k0:k1, :], src_fn(hf, k0, k1), self.sem(nm), reads=[], writes=[nm])
            tiles = [(s_, ci, tb) for s_ in range(ngr) for ci in range(nci) for tb in range(ntb)]

            def tile_ap(t):
                s_, ci, tb = t
                oc = s_ * nci + ci
                return self.hT[oc, :, t0 + tb * 512:t0 + (tb + 1) * 512], "hT:%d:%d" % (oc, (t0 // 512) + tb)

            def issue_load(i):
                ap, nm = tile_ap(tiles[i])
                p.dma("sp", hold[i % NHB], ap, self.sem("hold%d" % (i % NHB)), reads=[nm], writes=["hold%d" % (i % NHB)])

            def ssq_mm(i):
                s_, ci, tb = tiles[i]
                oc = s_ * nci + ci
                bi = ssq_banks[tb]
                p.op("pe", lambda e: e.matmul(self.banks[bi][:, :], ones[:], sqt[i % NHB][:, :],
                                              start=(oc == 0), stop=(oc == ngr * nci - 1)),
                     reads=["sqt%d" % (i % NHB), "onesb"], writes=["ps%d" % bi])

            PF = 2
            for i in range(min(PF, len(tiles))):
                issue_load(i)
            cur = None
            for i, (s_, ci, tb) in enumerate(tiles):
                if i + PF < len(tiles):
                    issue_load(i + PF)
                if ci == 0 and tb == 0:
                    cur = []
                    for (src, nel, kc0, nkc) in groups[s_]:
                        sv, sname = self.load_slab(src, nel)
                        cur.append((sv.rearrange("p (k n) -> p k n", n=SW), sname, kc0, nkc))
                bk, bname = self.bank()
                for (sv3, sname, kc0, nkc) in cur:
                    for kk in range(nkc):
                        kc = kc0 + kk
                        p.op("pe", lambda e, bk=bk, sv3=sv3, kk=kk, kc=kc, ci=ci, tb=tb: e.matmul(
                            bk[:, :], sv3[:, kk, ci * 128:(ci + 1) * 128], inT[:, kc, tb * 512:(tb + 1) * 512],
                            start=(kc == 0), stop=(kc == KC - 1)),
                            reads=[sname, in_names[kc]], writes=[bname])
                if i >= 2:
                    ssq_mm(i - 2)
                hb, hn = hold[i % NHB], "hold%d" % (i % NHB)
                p.op("dve", lambda e, hb=hb, bk=bk: e.tensor_tensor(hb[:, :], bk[:, :], hb[:, :], ALU.add),
                     reads=[bname, hn], writes=[hn])
                p.op("act", lambda e, hb=hb, i=i: e.activation(sqt[i % NHB][:, :], hb[:, :], AF.Square),
                     reads=[hn], writes=["sqt%d" % (i % NHB)])
                if xg_gcol is not None:
                    oc = s_ * nci + ci
                    tk = (t0 // 512) + tb
                    xgv = self.xg_chunk(oc)
                    p.op("act", lambda e, hb=hb, xgv=xgv, oc=oc, tk=tk: e.activation(
                        xgv[:, tk * 512:(tk + 1) * 512], hb[:, :], AF.Identity,
                        scale=self.cvt[:, xg_gcol + oc:xg_gcol + oc + 1]),
                        reads=[hn, "cvt"], writes=["xn:%d" % (2 * tk), "xn:%d" % (2 * tk + 1)])
                ap, nm = tile_ap(tiles[i])
                p.dma("sp", ap, hb, self.sem(hn + "st"), reads=[hn], writes=[nm])
            for i in range(max(0, len(tiles) - 2), len(tiles)):
                ssq_mm(i)
            for tb in range(ntb):
                bi = ssq_banks[tb]
                tk = (t0 // 512) + tb
                p.op("act", lambda e, bi=bi: e.activation(sdv, self.banks[bi][:, :], AF.Sqrt, bias=epsb[:, 0:1], scale=1.0 / D),
                     reads=["ps%d" % bi, "epsb"], writes=["sdl"])
                p.op("dve", lambda e, tk=tk: e.reciprocal(rstd[:, tk * 512:(tk + 1) * 512], sdv),
                     reads=["sdl"], writes=["rstd:%d" % tk])
            self.bank_pool = list(range(8))
            p.barrier()

    def phase_sc(self):
        p = self.p
        cvt = self.cvt
        xn = self.xn_view()
        o = SCR_OFF
        hhs = self.V(o, F32, [128, 2048]); o += 8192
        pp = self.V(o, F32, [128, 2050]); o += 8200
        cc = self.V(o, F32, [128, 2048]); o += 8192
        zt = [self.V(o + i * 4096, BF16, [128, 2048]) for i in range(2)]
        p.op("pool", lambda e: e.memset(pp[:, 0:1], 0.0), writes=["pp"])
        p.op("pool", lambda e: e.memset(pp[:, 2049:2050], 0.0), writes=["pp"])
        xn_names = ["xn:%d" % b for b in range(NBLK)]
        for s in range(8):
            sl_b = self.load_slab(self.sc_in[s], 4096)
            sl_c = self.load_slab(self.sc_in[8 + s], 4096)
            sl_h = self.load_slab(self.sc_in[16 + s], 4096)
            for ci in range(2):
                j = 2 * s + ci
                w0 = cvt[:, CV_SCW + 0 * 16 + j:CV_SCW + 0 * 16 + j + 1]
                w1 = cvt[:, CV_SCW + 1 * 16 + j:CV_SCW + 1 * 16 + j + 1]
                w2 = cvt[:, CV_SCW + 2 * 16 + j:CV_SCW + 2 * 16 + j + 1]

                def proj(slab, tb):
                    sv, sname = slab
                    sv3 = sv.rearrange("p (k n) -> p k n", n=256)
                    bk, bname = self.bank()
                    for kc in range(16):
                        p.op("pe", lambda e, bk=bk, sv3=sv3, kc=kc, ci=ci, tb=tb: e.matmul(
                            bk[:, :], sv3[:, kc, ci * 128:(ci + 1) * 128], xn[:, kc, tb * 512:(tb + 1) * 512],
                            start=(kc == 0), stop=(kc == 15)),
                            reads=[sname] + xn_names[2 * tb:2 * tb + 2], writes=[bname])
                    return bk, bname

                for tb in range(4):
                    bk, bname = proj(sl_h, tb)
                    p.op("act", lambda e, bk=bk, tb=tb: e.activation(hhs[:, tb * 512:(tb + 1) * 512], bk[:, :], AF.Copy),
                         reads=[bname], writes=["hhs"])
                for tb in range(4):
                    bk, bname = proj(sl_c, tb)
                    p.op("dve", lambda e, bk=bk, tb=tb: e.tensor_tensor(
                        pp[:, 1 + tb * 512:1 + (tb + 1) * 512], bk[:, :], hhs[:, tb * 512:(tb + 1) * 512], ALU.mult),
                        reads=[bname, "hhs"], writes=["pp"])
                p.op("act", lambda e, w1=w1: e.activation(cc[:, :], pp[:, 1:2049], AF.Identity, scale=w1),
                     reads=["pp", "cvt"], writes=["cc"])
                p.op("dve", lambda e, w0=w0: e.scalar_tensor_tensor(cc[:, :], pp[:, 0:2048], w0, cc[:, :], ALU.mult, ALU.add),
                     reads=["pp", "cc", "cvt"], writes=["cc"])
                p.op("dve", lambda e, w2=w2: e.scalar_tensor_tensor(cc[:, :], pp[:, 2:2050], w2, cc[:, :], ALU.mult, ALU.add),
                     reads=["pp", "cc", "cvt"], writes=["cc"])
                z, zname = zt[j % 2], "zt%d" % (j % 2)
                for tb in range(4):
                    bk, bname = proj(sl_b, tb)
                    p.op("dve", lambda e, bk=bk, tb=tb, z=z: e.tensor_tensor(
                        z[:, tb * 512:(tb + 1) * 512], bk[:, :], cc[:, tb * 512:(tb + 1) * 512], ALU.mult),
                        reads=[bname, "cc"], writes=[zname])
                p.dma("sp", self.mo[:, j, :], z, self.sem(zname), reads=[zname], writes=["mo:%d" % j])
        p.barrier()

    def phase_gqa(self):
        p = self.p
        cvt, ones, rotb, epsb = self.cvt, self.onesb, self.rotb, self.epsb
        xn = self.xn_view()
        xn_names = ["xn:%d" % b for b in range(NBLK)]
        Vt = self.V(65536, BF16, [128, 16, 512])
        o = SCR_OFF
        cos = self.V(o, F32, [128, S]); o += 8192
        sin = self.V(o, F32, [128, S]); o += 8192
        KT = self.V(o, BF16, [128, 4, S]); o += 16384
        tmps = []
        for i in range(3):
            d = {"i": i}
            d["sqb"] = self.V(o, BF16, [128, 512]); o += 1024
            d["knb"] = self.V(o, BF16, [128, 512]); o += 1024
            d["sd"] = self.V(o, F32, [128, 512]); o += 2048
            d["rs"] = self.V(o, F32, [128, 512]); o += 2048
            d["t1"] = self.V(o, F32, [128, 512]); o += 2048
            d["t2"] = self.V(o, F32, [128, 512]); o += 2048
            tmps.append(d)
        qt = [self.V(81920 + i * 1024, BF16, [128, 512]) for i in range(3)]
        assert o <= M_BYTES
        p.dma("sp", cos, self.cs[0], self.sem("cos"), writes=["cos"])
        p.dma("sp", sin, self.cs[1], self.sem("sin"), writes=["sin"])

        for vs in range(2):
            sv, sname = self.load_slab(self.gq_qkv[10 + vs], 4096)
            sv3 = sv.rearrange("p (k n) -> p k n", n=256)
            for tc in range(16):
                bk, bname = self.bank()
                for kc in range(16):
                    p.op("pe", lambda e, bk=bk, sv3=sv3, kc=kc, tc=tc: e.matmul(
                        bk[:, 0:256], xn[:, kc, tc * 128:(tc + 1) * 128], sv3[:, kc, :],
                        start=(kc == 0), stop=(kc == 15)),
                        reads=[sname, xn_names[tc // 2]], writes=[bname])
                p.op("act", lambda e, bk=bk, tc=tc, vs=vs: e.activation(
                    Vt[:, tc, vs * 256:(vs + 1) * 256], bk[:, 0:256], AF.Copy),
                    reads=[bname], writes=["Vt"])

        tiles = []
        for ks in range(2):
            for ci in range(2):
                for tb in range(4):
                    tiles.append(dict(slab=8 + ks, ci=ci, tb=tb, gcol=CV_KN,
                                      dst=KT[:, 2 * ks + ci, tb * 512:(tb + 1) * 512], dname="KT", store=None))
        nq = 0
        for q_s in range(8):
            for ci in range(2):
                for tb in range(4):
                    tiles.append(dict(slab=q_s, ci=ci, tb=tb, gcol=CV_QN, dst=qt[nq % 3], dname="qt%d" % (nq % 3),
                                      store=(2 * q_s + ci, tb)))
                    nq += 1
        slabs = {}

        slab_order = [8, 9] + list(range(8))

        def ensure(si):
            if si < len(slab_order) and slab_order[si] not in slabs:
                sv, sname = self.load_slab(self.gq_qkv[slab_order[si]], 4096)
                slabs[slab_order[si]] = (sv.rearrange("p (k n) -> p k n", n=256), sname)

        def stage_a(t, i):
            si = slab_order.index(t["slab"])
            ensure(si)
            ensure(si + 1)
            sv3, sname = slabs[t["slab"]]
            d = t["d"] = tmps[i % 3]
            ti = d["i"]
            ci, tb = t["ci"], t["tb"]
            bk, bname = self.bank()
            t["bk"], t["bname"] = bk, bname
            for kc in range(16):
                p.op("pe", lambda e, kc=kc: e.matmul(
                    bk[:, :], sv3[:, kc, ci * 128:(ci + 1) * 128], xn[:, kc, tb * 512:(tb + 1) * 512],
                    start=(kc == 0), stop=(kc == 15)),
                    reads=[sname] + xn_names[2 * tb:2 * tb + 2], writes=[bname])
            p.op("act", lambda e: e.activation(d["sqb"], bk[:, :], AF.Square), reads=[bname], writes=["sqb%d" % ti])

        def stage_b(t):
            d = t["d"]
            ti = d["i"]
            bk, bname, gcol = t["bk"], t["bname"], t["gcol"]
            b2, b2n = self.bank()
            p.op("pe", lambda e: e.matmul(b2[:, :], ones[:], d["sqb"], start=True, stop=True),
                 reads=["sqb%d" % ti, "onesb"], writes=[b2n])
            p.op("act", lambda e: e.activation(d["sd"], b2[:, :], AF.Sqrt, bias=epsb[:, 0:1], scale=1.0 / 128),
                 reads=[b2n, "epsb"], writes=["sd%d" % ti])
            p.op("dve", lambda e: e.reciprocal(d["rs"], d["sd"]), reads=["sd%d" % ti], writes=["rs%d" % ti])
            p.op("dve", lambda e: e.scalar_tensor_tensor(d["knb"], bk[:, :], cvt[:, gcol:gcol + 1], d["rs"],
                                                         ALU.mult, ALU.mult),
                 reads=[bname, "rs%d" % ti, "cvt"], writes=["knb%d" % ti])

        def stage_c(t):
            d = t["d"]
            ti = d["i"]
            tb, dst, dname = t["tb"], t["dst"], t["dname"]
            b3, b3n = self.bank()
            p.op("pe", lambda e: e.matmul(b3[:, :], rotb[:], d["knb"], start=True, stop=True),
                 reads=["knb%d" % ti, "rotb"], writes=[b3n])
            p.op("pool", lambda e: e.tensor_tensor(d["t1"], d["knb"], cos[:, tb * 512:(tb + 1) * 512], ALU.mult),
                 reads=["knb%d" % ti, "cos"], writes=["t1%d" % ti])
            p.op("dve", lambda e: e.tensor_tensor(d["t2"], b3[:, :], sin[:, tb * 512:(tb + 1) * 512], ALU.mult),
                 reads=[b3n, "sin"], writes=["t2%d" % ti])
            p.op("dve", lambda e: e.tensor_tensor(dst, d["t1"], d["t2"], ALU.add),
                 reads=["t1%d" % ti, "t2%d" % ti], writes=[dname])
            if t["store"] is not None:
                h, tb_ = t["store"]
                p.dma("sp", self.qs[:, h, tb_ * 512:(tb_ + 1) * 512], dst, self.sem(dname), reads=[dname],
                      writes=["qs:%d:%d" % (h, tb_)])

        nt = len(tiles)
        for i in range(nt + 2):
            if i < nt:
                stage_a(tiles[i], i)
            if 0 <= i - 1 < nt:
                stage_b(tiles[i - 1])
            if 0 <= i - 2 < nt:
                stage_c(tiles[i - 2])
        p.barrier()

        qg = [self.V(i * 16384, BF16, [128, 4, S]) for i in range(2)]
        PT = [self.V(32768 + i * 1024, BF16, [128, 512]) for i in range(4)]
        rc = [self.V(36864 + i * 2048, F32, [128, 512]) for i in range(2)]
        ot = [self.V(40960 + i * 1024, BF16, [128, 512]) for i in range(2)]
        scale = 128.0 ** -0.5
        def attend(g, qh, qb, it, qv, qn):
            h = 4 * g + qh
            bo, bon = self.banks[(it % 2) * 2], "ps%d" % ((it % 2) * 2)
            bs, bsn = self.banks[(it % 2) * 2 + 1], "ps%d" % ((it % 2) * 2 + 1)
            rcv, rcn = rc[it % 2], "rc%d" % (it % 2)
            otv, otn = ot[it % 2], "ot%d" % (it % 2)

            def s_step(kc):
                sb_i = 4 + (kc % 4)
                sbk, sbn = self.banks[sb_i], "ps%d" % sb_i
                ptv, ptn = PT[kc % 4], "PT%d" % (kc % 4)
                p.op("pe", lambda e: e.matmul(
                    sbk[:, :], KT[:, g, kc * 128:(kc + 1) * 128], qv[:, qh, qb * 512:(qb + 1) * 512],
                    start=True, stop=True), reads=["KT", qn], writes=[sbn])
                p.op("act", lambda e: e.activation(ptv, sbk[:, :], AF.Exp, scale=scale),
                     reads=[sbn], writes=[ptn])

            def pv_step(kc):
                ptv, ptn = PT[kc % 4], "PT%d" % (kc % 4)
                p.op("pe", lambda e: e.matmul(
                    bo[:, :], Vt[:, kc, g * 128:(g + 1) * 128], ptv, start=(kc == 0), stop=(kc == 15)),
                    reads=["Vt", ptn], writes=[bon])
                p.op("pe", lambda e: e.matmul(
                    bs[:, :], ones[:], ptv, start=(kc == 0), stop=(kc == 15)),
                    reads=["onesb", ptn], writes=[bsn])

            s_step(0)
            s_step(1)
            for kc in range(16):
                if kc + 2 < 16:
                    s_step(kc + 2)
                pv_step(kc)
            p.op("dve", lambda e: e.reciprocal(rcv, bs[:, :]), reads=[bsn], writes=[rcn])
            p.op("dve", lambda e: e.tensor_tensor(otv, bo[:, :], rcv, ALU.mult), reads=[bon, rcn], writes=[otn])
            p.dma("sp", self.mo[:, h, qb * 512:(qb + 1) * 512], otv, self.sem(otn), reads=[otn],
                  writes=["mo:%d:%d" % (h, qb)])

        it = 0
        for g in range(4):
            qv, qn = qg[g % 2], "qg%d" % (g % 2)
            p.dma("sp", qv, self.qs[:, 4 * g:4 * g + 4, :], self.sem(qn), reads=[], writes=[qn])
            for qh in range(4):
                for qb in range(4):
                    attend(g, qh, qb, it, qv, qn)
                    it += 1
        p.barrier()

    def phase_na(self, j):
        p = self.p
        ones, identb = self.onesb, self.identb
        xn = self.xn_view()
        xn_names = ["xn:%d" % b for b in range(NBLK)]
        Vt = [self.V(65536 + i * 8192, BF16, [128, 16, 256]) for i in range(2)]
        o = SCR_OFF
        qT = [self.V(o + i * 4096, BF16, [128, S]) for i in range(2)]; o += 8192
        kT = [self.V(o + i * 4096, BF16, [128, S]) for i in range(2)]; o += 8192
        bias = [self.V(o + i * 5376, BF16, [128, NA_NTAB * 128]) for i in range(2)]; o += 10752
        PT = [self.V(o + i * 1280, BF16, [128, 640]) for i in range(2)]; o += 2560
        rc = [self.V(o + i * 512, F32, [128, 128]) for i in range(2)]; o += 1024
        at = [self.V(o + i * 4096, BF16, [128, S]) for i in range(2)]; o += 8192
        scale = 128.0 ** -0.5
        banks = self.banks
        pj = [0]

        def proj_bank():
            i = pj[0] % 2
            pj[0] += 1
            return banks[i], "ps%d" % i

        def head(hp, ci, slq, slk, vt, vtn):
            h = 2 * hp + ci
            qv, qn = qT[h % 2], "qT%d" % (h % 2)
            kv, kn = kT[h % 2], "kT%d" % (h % 2)
            bv, bn = bias[h % 2], "bias%d" % (h % 2)
            av, an = at[h % 2], "at%d" % (h % 2)
            p.dma("pool", bv, self.na_bias[j, h], self.sem(bn), reads=[], writes=[bn])
            for (slab, dst, dn, isq) in ((slq, qv, qn, True), (slk, kv, kn, False)):
                sv, sname = slab
                sv3 = sv.rearrange("p (k n) -> p k n", n=256)
                for tb in range(4):
                    bk, bname = proj_bank()
                    for kc in range(16):
                        p.op("pe", lambda e, bk=bk, sv3=sv3, kc=kc, tb=tb: e.matmul(
                            bk[:, :], sv3[:, kc, ci * 128:(ci + 1) * 128], xn[:, kc, tb * 512:(tb + 1) * 512],
                            start=(kc == 0), stop=(kc == 15)),
                            reads=[sname] + xn_names[2 * tb:2 * tb + 2], writes=[bname])
                    if isq:
                        p.op("act", lambda e, bk=bk, dst=dst, tb=tb: e.activation(
                            dst[:, tb * 512:(tb + 1) * 512], bk[:, :], AF.Copy, scale=scale),
                            reads=[bname], writes=[dn])
                    else:
                        p.op("dve", lambda e, bk=bk, dst=dst, tb=tb: e.tensor_copy(
                            dst[:, tb * 512:(tb + 1) * 512], bk[:, :]), reads=[bname], writes=[dn])

            def qk(m):
                tbase, nch, kc0 = _na_pair(m)
                A, An = banks[2 + m % 2], "ps%d" % (2 + m % 2)
                B, Bn = banks[4 + m % 2], "ps%d" % (4 + m % 2)
                ptv, ptn = PT[m % 2], "PT%d" % (m % 2)
                for c in range(nch):
                    dst = A[:, c * 128:(c + 1) * 128] if c < 4 else B[:, 0:128]
                    dn = An if c < 4 else Bn
                    p.op("pe", lambda e, dst=dst, c=c: e.matmul(
                        dst, kv[:, (kc0 + c) * 128:(kc0 + c + 1) * 128], qv[:, m * 128:(m + 1) * 128],
                        start=True, stop=False), reads=[kn, qn], writes=[dn])
                    p.op("pe", lambda e, dst=dst, c=c: e.matmul(
                        dst, identb[:], bv[:, (tbase + c) * 128:(tbase + c + 1) * 128],
                        start=False, stop=True), reads=["identb", bn], writes=[dn])
                p.op("act", lambda e: e.activation(ptv[:, 0:512], A[:, :], AF.Exp), reads=[An], writes=[ptn])
                if nch == 5:
                    p.op("act", lambda e: e.activation(ptv[:, 512:640], B[:, 0:128], AF.Exp), reads=[Bn], writes=[ptn])

            def pv(m):
                tbase, nch, kc0 = _na_pair(m)
                O, On = banks[6 + m % 2], "ps%d" % (6 + m % 2)
                ptv, ptn = PT[m % 2], "PT%d" % (m % 2)
                rcv, rcn = rc[m % 2], "rc%d" % (m % 2)
                for c in range(nch):
                    p.op("pe", lambda e, c=c: e.matmul(
                        O[:, 0:128], vt[:, kc0 + c, ci * 128:(ci + 1) * 128], ptv[:, c * 128:(c + 1) * 128],
                        start=(c == 0), stop=(c == nch - 1)), reads=[vtn, ptn], writes=[On])
                for c in range(nch):
                    p.op("pe", lambda e, c=c: e.matmul(
                        O[:, 128:256], ones[:], ptv[:, c * 128:(c + 1) * 128],
                        start=(c == 0), stop=(c == nch - 1)), reads=["onesb", ptn], writes=[On])
                p.op("dve", lambda e: e.reciprocal(rcv, O[:, 128:256]), reads=[On], writes=[rcn])
                p.op("dve", lambda e: e.tensor_tensor(av[:, m * 128:(m + 1) * 128], O[:, 0:128], rcv, ALU.mult),
                     reads=[On, rcn], writes=[an])

            qk(0)
            for m in range(16):
                if m + 1 < 16:
                    qk(m + 1)
                pv(m)
            p.dma("sp", self.mo[:, h, :], av, self.sem(an), reads=[an], writes=["mo:%d" % h])

        for hp in range(8):
            slq = self.load_slab(self.na_qkv[j, hp], 4096)
            slk = self.load_slab(self.na_qkv[j, 8 + hp], 4096)
            slv = self.load_slab(self.na_qkv[j, 16 + hp], 4096)
            vt, vtn = Vt[hp % 2], "Vt%d" % (hp % 2)
            sv3 = slv[0].rearrange("p (k n) -> p k n", n=256)
            for tc in range(16):
                bk, bname = proj_bank()
                for kc in range(16):
                    p.op("pe", lambda e, bk=bk, sv3=sv3, kc=kc, tc=tc: e.matmul(
                        bk[:, 0:256], xn[:, kc, tc * 128:(tc + 1) * 128], sv3[:, kc, :],
                        start=(kc == 0), stop=(kc == 15)),
                        reads=[slv[1], xn_names[tc // 2]], writes=[bname])
                p.op("act", lambda e, bk=bk, tc=tc, vt=vt: e.activation(vt[:, tc, :], bk[:, 0:256], AF.Copy),
                     reads=[bname], writes=[vtn])
            for ci in range(2):
                head(hp, ci, slq, slk, vt, vtn)
        p.barrier()

    def build(self):
        cfg = self.cfg
        layers = cfg.get("layers", [0, 1, 2, 3])
        do_mixer, do_ffn = cfg.get("mixer", True), cfg.get("ffn", True)
        self.consts()
        first = True
        for l in layers:
            m, j = l % 3, l // 3
            fused = False
            if do_mixer:
                if first:
                    self.phase_input(CV_MIX + 16 * l)
                else:
                    self.phase_norm(CV_MIX + 16 * l, barrier=(m != 1))
                first = False
                fused = do_ffn
                xg = (CV_FFN + 16 * l) if fused else None
                if m == 0:
                    self.phase_na(j)
                    self.phase_linear([[(self.na_o[j, s_], 4096, 0, 16)] for s_ in range(8)], 16, 256, self.src_mo, 1, xg)
                elif m == 1:
                    self.phase_sc()
                    self.phase_linear([[(self.sc_o[s_], 4096, 0, 16)] for s_ in range(8)], 16, 256, self.src_mo, 1, xg)
                else:
                    self.phase_gqa()
                    self.phase_linear([[(self.gq_o[s_], 4096, 0, 16)] for s_ in range(8)], 16, 256, self.src_mo, 1, xg)
            if do_ffn:
                if not fused:
                    if first:
                        self.phase_input(CV_FFN + 16 * l)
                    else:
                        self.phase_norm(CV_FFN + 16 * l, barrier=False)
                    first = False
                self.phase_ffn_up(l, fused=fused)
                self.phase_linear([[(self.w_dn[l, s_, kh], 2816, 22 * kh, 22) for kh in range(2)] for s_ in range(16)],
                                  NCHF, 128, self.src_hm, 2)
        if first:
            self.phase_input(CV_FIN)
        if cfg.get("dump_h"):
            self.phase_dump()
        self.phase_output(CV_FIN)
        self.p.finish()
        self.es.close()
        return self.nc

    def src_mo(self, hf, k0, k1):
        return self.mo[:, k0:k1, :]

    def src_hm(self, hf, k0, k1):
        return self.hm[hf, :, k0:k1, :]

    def phase_dump(self):
        p = self.p
        for c in range(16):
            p.dma("sp", self.hdump[c], self.hT[c], self.sem("dump%d" % c), reads=[], writes=["dump%d" % c])
        p.barrier()


def _tile_w(W, sw):
    K, N = W.shape
    kc = K // 128
    t = W.reshape(kc, 128, N // sw, sw).transpose(2, 1, 0, 3)
    return np.ascontiguousarray(t).reshape(N // sw, 128, kc * sw)


def _tile_wdn(W):
    t = W.reshape(2, 22, 128, 16, 128).transpose(3, 0, 2, 1, 4)
    return np.ascontiguousarray(t).reshape(16, 2, 128, 22 * 128)


def _fm(v):
    v = np.asarray(v, np.float32)
    lead = v.shape[:-1]
    c = v.shape[-1] // 128
    t = v.reshape(*lead, c, 128)
    t = np.moveaxis(t, -1, 0)
    return np.ascontiguousarray(t).reshape(128, -1)


def _na_bias_tables(rpb):
    H = rpb.shape[0]
    out = np.empty((H, 128, NA_NTAB, 128), np.float32)
    pk = np.arange(128)[:, None]
    pq = np.arange(128)[None, :]
    specs = [((0, 1), 0, 4, 0), ((2, 3), 0, 4, 4), ((4, 5), 0, 5, 8), ((28, 29), 24, 4, 13), ((30, 31), 24, 4, 17)]
    for (r0, r1), a, nch, base in specs:
        r = np.where(pq < 64, r0, r1)
        qcol = pq % 64
        r_start = np.clip(r - 4, 0, 24)
        c_start = np.clip(qcol - 8, 0, 48)
        for c in range(nch):
            krow = a + 2 * c + pk // 64
            kcol = pk % 64
            valid = (krow >= r_start) & (krow < r_start + 8) & (kcol >= c_start) & (kcol < c_start + 16)
            dr = np.clip(krow - r + 7, 0, 14)
            dc = np.clip(kcol - qcol + 15, 0, 30)
            g = rpb[:, dr, dc]
            out[:, :, base + c, :] = np.where(valid[None], g, np.float32(-30000.0))
    return out.reshape(H, 128, NA_NTAB * 128)


def _rope_tables():
    t = np.arange(S)
    row = (t // 64).astype(np.float32)[:, None]
    col = (t % 64).astype(np.float32)[:, None]
    inv = (np.float32(10000.0) ** (-np.arange(0, 64, 2, dtype=np.float32) / np.float32(64))).astype(np.float32)
    ang = np.concatenate([row * inv, row * inv, col * inv, col * inv], axis=-1)
    return np.ascontiguousarray(np.stack([np.cos(ang).T, np.sin(ang).T]).astype(np.float32))


def _rot_matrix():
    R = np.zeros((128, 128), np.float32)
    for do in range(128):
        if (do % 64) < 32:
            R[do + 32, do] = -1.0
        else:
            R[do - 32, do] = 1.0
    return R


def prep_shared(inp):
    f = lambda a: np.asarray(a, np.float32)
    sh = {}
    cvp = [_fm(f(inp["mix_norm"])), _fm(f(inp["ffn_norm"])), _fm(f(inp["final_norm"])),
           _fm(f(inp["ffn_conv_w"])), _fm(f(inp["ffn_conv_b"])), _fm(f(inp["sc_conv_w"])[0]),
           f(inp["gqa_q_norm"])[0].reshape(128, 1), f(inp["gqa_k_norm"])[0].reshape(128, 1)]
    sh["cv"] = np.ascontiguousarray(np.concatenate(cvp, axis=1))
    assert sh["cv"].shape == (128, NCV), sh["cv"].shape
    sh["ident"] = np.eye(128, dtype=np.float32)
    sh["rot"] = _rot_matrix()
    sh["cs"] = _rope_tables()
    sh["w_up"] = np.stack([_tile_w(f(inp["ffn_w_up"])[l], 256) for l in range(4)])
    sh["w_dn"] = np.stack([_tile_wdn(f(inp["ffn_w_down"])[l]) for l in range(4)])
    sh["na_qkv"] = np.stack([_tile_w(f(inp["na_w_qkv"])[j], 256) for j in range(2)])
    sh["na_o"] = np.stack([_tile_w(f(inp["na_w_o"])[j], 256) for j in range(2)])
    sh["na_bias"] = np.stack([_na_bias_tables(f(inp["na_rpb"])[j]) for j in range(2)])
    sh["sc_in"] = _tile_w(f(inp["sc_w_in"])[0], 256)
    sh["sc_o"] = _tile_w(f(inp["sc_w_out"])[0], 256)
    sh["gq_qkv"] = _tile_w(f(inp["gqa_w_qkv"])[0], 256)
    sh["gq_o"] = _tile_w(f(inp["gqa_w_o"])[0], 256)
    return sh


_CFG = {"layers": [0, 1, 2, 3]}


def kernel(**inputs):
    x = np.asarray(inputs["x"], np.float32)
    sh = prep_shared(inputs)
    nc = Builder(dict(_CFG)).build()
    n = x.shape[0]
    in_maps = [dict(sh, x=np.ascontiguousarray(x[b])) for b in range(n)]
    res = run_bass_kernel_spmd(nc, in_maps, core_ids=list(range(n)))
    return np.stack([np.asarray(r["out"], np.float32) for r in res.results], axis=0)
```
